# Optimizing a Trainium2 kernel written in Bass

```python
import math
import jax, jax.numpy as jnp
from jax import lax
import numpy as np

D_MODEL = 2048
BATCH = 4
SEQ = 2048
DEPTH = 2
DEC_BATCH = 128
DEC_SEQ = 4
PAST_LEN = 2048
PAGE_SIZE = 128

D_MIX = D_MODEL
D_ATTN = D_MIX // 2
HEAD_DIM = 128
N_HEADS = D_ATTN // HEAD_DIM
D_SSM = D_MIX - D_ATTN
SSM_GROUP = 16
N_SSM_GROUPS = D_SSM // SSM_GROUP
SSM_STATE = 64
D_IN = 3 * D_ATTN + N_HEADS + D_SSM
POOL_WINDOWS = (2, 4, 8, 16)
POOL_GROUP = D_MODEL // len(POOL_WINDOWS)
POOL_BUF = max(POOL_WINDOWS) - 1
D_FF = ((-(-8 * D_MODEL // 3) + 255) // 256) * 256
Q_BLOCK = 128
RMS_EPS = 1e-6
DT_MIN = 1e-3
DT_MAX = 1e-1

kernel_name = 'fox_s5_pool_hybrid_step'

F32 = jnp.float32


def rmsnorm(x, w):
    xf = x.astype(F32)
    y = xf * lax.rsqrt(jnp.mean(xf * xf, axis=-1, keepdims=True) + RMS_EPS)
    return (y * w.astype(F32)).astype(x.dtype)


def swiglu(h, w1, w3, w2):
    return (jax.nn.silu(h @ w1) * (h @ w3)) @ w2


def mixer_ab_projection(h, w_in, b_f, q_gain, k_gain):
    B, L, _ = h.shape
    z = h @ w_in
    q = z[..., :D_ATTN].reshape(B, L, N_HEADS, HEAD_DIM)
    k = z[..., D_ATTN:2 * D_ATTN].reshape(B, L, N_HEADS, HEAD_DIM)
    v = z[..., 2 * D_ATTN:3 * D_ATTN].reshape(B, L, N_HEADS, HEAD_DIM)
    f = z[..., 3 * D_ATTN:3 * D_ATTN + N_HEADS]
    u = z[..., 3 * D_ATTN + N_HEADS:]
    q = rmsnorm(q, q_gain)
    k = rmsnorm(k, k_gain)
    logf = jax.nn.log_sigmoid(f.astype(F32) + b_f.astype(F32))
    return q, k, v, logf, u


def fox_attend_prompt(q, k, v, logf):
    B, L, H, Dh = q.shape
    scale = Dh ** -0.5
    c = jnp.cumsum(logf, axis=1)
    c_t = c.transpose(0, 2, 1)
    k_pos = jnp.arange(L)

    def block(i):
        s0 = i * Q_BLOCK
        qb = lax.dynamic_slice_in_dim(q, s0, Q_BLOCK, axis=1)
        cb = lax.dynamic_slice_in_dim(c_t, s0, Q_BLOCK, axis=2)
        q_pos = s0 + jnp.arange(Q_BLOCK)
        logits = jnp.einsum('bqhd,bkhd->bhqk', qb, k, preferred_element_type=F32) * scale
        logits = logits + (cb[..., :, None] - c_t[..., None, :])
        logits = jnp.where(k_pos[None, :] <= q_pos[:, None], logits, -jnp.inf)
        p = jax.nn.softmax(logits, axis=-1).astype(v.dtype)
        return jnp.einsum('bhqk,bkhd->bqhd', p, v)

    out = lax.map(block, jnp.arange(L // Q_BLOCK))
    return out.transpose(1, 0, 2, 3, 4).reshape(B, L, H * Dh)


def fox_attend_sample(q, k_new, v_new, logf_new, k_past, v_past, logf_past):
    B, Q, H, Dh = q.shape
    P = k_past.shape[1]
    scale = Dh ** -0.5
    c_past = jnp.cumsum(logf_past.astype(F32), axis=1)
    c_new = c_past[:, -1:] + jnp.cumsum(logf_new.astype(F32), axis=1)
    cp = c_past.transpose(0, 2, 1)
    cn = c_new.transpose(0, 2, 1)
    s_past = jnp.einsum('bqhd,bkhd->bhqk', q, k_past, preferred_element_type=F32) * scale
    s_past = s_past + (cn[..., :, None] - cp[..., None, :])
    s_new = jnp.einsum('bqhd,bkhd->bhqk', q, k_new, preferred_element_type=F32) * scale
    s_new = s_new + (cn[..., :, None] - cn[..., None, :])
    causal = jnp.tril(jnp.ones((Q, Q), dtype=bool))
    s_new = jnp.where(causal, s_new, -jnp.inf)
    p = jax.nn.softmax(jnp.concatenate([s_past, s_new], axis=-1), axis=-1).astype(v_new.dtype)
    out = (jnp.einsum('bhqk,bkhd->bqhd', p[..., :P], v_past)
           + jnp.einsum('bhqk,bkhd->bqhd', p[..., P:], v_new))
    return out.reshape(B, Q, H * Dh)


def _complex_affine_combine(left, right):
    a1r, a1i, b1r, b1i = left
    a2r, a2i, b2r, b2i = right
    return (a1r * a2r - a1i * a2i,
            a1r * a2i + a1i * a2r,
            a2r * b1r - a2i * b1i + b2r,
            a2r * b1i + a2i * b1r + b2i)


def s5_mixer(u, h0_re, h0_im, lam_re, lam_im, log_dt, b_re, b_im, c_re, c_im, d, w_glu, b_glu):
    B, L, _ = u.shape
    uf = u.astype(F32).reshape(B, L, N_SSM_GROUPS, SSM_GROUP)
    dt = jnp.exp(log_dt.astype(F32))[:, None]
    lr, li = lam_re.astype(F32), lam_im.astype(F32)
    mag = jnp.exp(lr * dt)
    ar, ai = mag * jnp.cos(li * dt), mag * jnp.sin(li * dt)
    den = lr * lr + li * li
    zr = ((ar - 1.0) * lr + ai * li) / den
    zi = (ai * lr - (ar - 1.0) * li) / den
    br, bi = b_re.astype(F32), b_im.astype(F32)
    bbr = zr[..., None] * br - zi[..., None] * bi
    bbi = zr[..., None] * bi + zi[..., None] * br
    xr = jnp.einsum('blgc,gpc->blgp', uf, bbr)
    xi = jnp.einsum('blgc,gpc->blgp', uf, bbi)
    h0r, h0i = h0_re.astype(F32), h0_im.astype(F32)
    xr = xr.at[:, 0].add(ar * h0r - ai * h0i)
    xi = xi.at[:, 0].add(ar * h0i + ai * h0r)
    a_r = jnp.broadcast_to(ar, xr.shape)
    a_i = jnp.broadcast_to(ai, xr.shape)
    _, _, hr, hi = lax.associative_scan(_complex_affine_combine, (a_r, a_i, xr, xi), axis=1)
    y = (jnp.einsum('blgp,gcp->blgc', hr, c_re.astype(F32))
         - jnp.einsum('blgp,gcp->blgc', hi, c_im.astype(F32))
         + d.astype(F32) * uf)
    y = y.reshape(B, L, D_SSM)
    z = jax.nn.gelu(y)
    out = z * jax.nn.sigmoid(z @ w_glu.astype(F32) + b_glu.astype(F32))
    return out.astype(u.dtype), hr[:, -1], hi[:, -1]


def pool_mixer(hp, pos, pool_w, pool_scale):
    L = hp.shape[1] - POOL_BUF
    s = jnp.cumsum(hp.astype(F32), axis=1)
    s = jnp.pad(s, ((0, 0), (1, 0), (0, 0)))
    cur = hp[:, POOL_BUF:].astype(F32)
    outs = []
    for g, w in enumerate(POOL_WINDOWS):
        lo_c, hi_c = g * POOL_GROUP, (g + 1) * POOL_GROUP
        upper = s[:, POOL_BUF + 1:POOL_BUF + 1 + L, lo_c:hi_c]
        lower = s[:, POOL_BUF + 1 - w:POOL_BUF + 1 - w + L, lo_c:hi_c]
        cnt = jnp.minimum(pos + 1, w).astype(F32)[None, :, None]
        pooled = (upper - lower) / cnt - cur[..., lo_c:hi_c]
        outs.append(pooled @ pool_w[g].astype(F32))
    out = jnp.concatenate(outs, axis=-1) * pool_scale.astype(F32)
    return out.astype(hp.dtype)


def setup_inputs(seed: int = 0) -> dict:
    key = jax.random.key(seed)
    ks = jax.random.split(key, 32)
    n_attn = (DEPTH + 1) // 2
    n_pool = DEPTH // 2
    n_pages = PAST_LEN // PAGE_SIZE
    n_phys = (5 * DEC_BATCH * n_pages + 3) // 4
    nrm = jax.random.normal
    x_prompt = nrm(ks[0], (BATCH, SEQ, D_MODEL), F32)
    x_sample = nrm(ks[1], (DEC_BATCH, DEC_SEQ, D_MODEL), F32)
    cache_k = nrm(ks[2], (n_attn, n_phys, PAGE_SIZE, N_HEADS, HEAD_DIM), F32)
    cache_v = nrm(ks[3], (n_attn, n_phys, PAGE_SIZE, N_HEADS, HEAD_DIM), F32)
    cache_logf = jax.nn.log_sigmoid(3.0 + nrm(ks[4], (n_attn, n_phys, PAGE_SIZE, N_HEADS), F32))
    state_s5_re = 0.1 * nrm(ks[5], (n_attn, DEC_BATCH, N_SSM_GROUPS, SSM_STATE), F32)
    state_s5_im = 0.1 * nrm(ks[6], (n_attn, DEC_BATCH, N_SSM_GROUPS, SSM_STATE), F32)
    state_pool = nrm(ks[7], (n_pool, DEC_BATCH, POOL_BUF, D_MODEL), F32)
    page_table = jax.random.permutation(ks[8], n_phys)[:DEC_BATCH * n_pages].reshape(DEC_BATCH, n_pages).astype(jnp.int32)
    norm_mix_w = 1.0 + 0.02 * nrm(ks[9], (DEPTH, D_MODEL), F32)
    norm_ffn_w = 1.0 + 0.02 * nrm(ks[10], (DEPTH, D_MODEL), F32)
    w_in = nrm(ks[11], (n_attn, D_MODEL, D_IN), F32) * D_MODEL ** -0.5
    b_f = 3.0 + 0.5 * nrm(ks[12], (n_attn, N_HEADS), F32)
    q_norm_w = 1.0 + 0.02 * nrm(ks[13], (n_attn, HEAD_DIM), F32)
    k_norm_w = 1.0 + 0.02 * nrm(ks[14], (n_attn, HEAD_DIM), F32)
    s5_lam_re = -0.5 + 0.01 * nrm(ks[15], (n_attn, N_SSM_GROUPS, SSM_STATE), F32)
    s5_lam_im = jnp.pi * jnp.arange(SSM_STATE, dtype=F32) + 0.01 * nrm(ks[16], (n_attn, N_SSM_GROUPS, SSM_STATE), F32)
    s5_log_dt = jax.random.uniform(ks[17], (n_attn, N_SSM_GROUPS), F32, math.log(DT_MIN), math.log(DT_MAX))
    s5_b_re = nrm(ks[18], (n_attn, N_SSM_GROUPS, SSM_STATE, SSM_GROUP), F32) * (2 * SSM_GROUP) ** -0.5
    s5_b_im = nrm(ks[19], (n_attn, N_SSM_GROUPS, SSM_STATE, SSM_GROUP), F32) * (2 * SSM_GROUP) ** -0.5
    s5_c_re = nrm(ks[20], (n_attn, N_SSM_GROUPS, SSM_GROUP, SSM_STATE), F32) * SSM_STATE ** -0.5
    s5_c_im = nrm(ks[21], (n_attn, N_SSM_GROUPS, SSM_GROUP, SSM_STATE), F32) * SSM_STATE ** -0.5
    s5_d = nrm(ks[22], (n_attn, N_SSM_GROUPS, SSM_GROUP), F32)
    w_glu = nrm(ks[23], (n_attn, D_SSM, D_SSM), F32) * D_SSM ** -0.5
    b_glu = 0.02 * nrm(ks[24], (n_attn, D_SSM), F32)
    w_out = nrm(ks[25], (n_attn, D_MIX, D_MODEL), F32) * D_MIX ** -0.5
    pool_w = nrm(ks[26], (n_pool, len(POOL_WINDOWS), POOL_GROUP, POOL_GROUP), F32) * POOL_GROUP ** -0.5
    pool_scale = 1.0 + 0.02 * nrm(ks[27], (n_pool, D_MODEL), F32)
    ffn_w1 = nrm(ks[28], (DEPTH, D_MODEL, D_FF), F32) * D_MODEL ** -0.5
    ffn_w3 = nrm(ks[29], (DEPTH, D_MODEL, D_FF), F32) * D_MODEL ** -0.5
    ffn_w2 = nrm(ks[30], (DEPTH, D_FF, D_MODEL), F32) * D_FF ** -0.5
    return {'x_prompt': x_prompt, 'x_sample': x_sample, 'cache_k': cache_k, 'cache_v': cache_v,
            'cache_logf': cache_logf, 'state_s5_re': state_s5_re, 'state_s5_im': state_s5_im,
            'state_pool': state_pool, 'page_table': page_table,
            'norm_mix_w': norm_mix_w, 'norm_ffn_w': norm_ffn_w, 'w_in': w_in, 'b_f': b_f,
            'q_norm_w': q_norm_w, 'k_norm_w': k_norm_w, 's5_lam_re': s5_lam_re, 's5_lam_im': s5_lam_im,
            's5_log_dt': s5_log_dt, 's5_b_re': s5_b_re, 's5_b_im': s5_b_im, 's5_c_re': s5_c_re,
            's5_c_im': s5_c_im, 's5_d': s5_d, 'w_glu': w_glu, 'b_glu': b_glu, 'w_out': w_out,
            'pool_w': pool_w, 'pool_scale': pool_scale, 'ffn_w1': ffn_w1, 'ffn_w3': ffn_w3, 'ffn_w2': ffn_w2}


def reference(x_prompt, x_sample, cache_k, cache_v, cache_logf, state_s5_re, state_s5_im, state_pool,
              page_table, norm_mix_w, norm_ffn_w, w_in, b_f, q_norm_w, k_norm_w, s5_lam_re, s5_lam_im,
              s5_log_dt, s5_b_re, s5_b_im, s5_c_re, s5_c_im, s5_d, w_glu, b_glu, w_out, pool_w,
              pool_scale, ffn_w1, ffn_w3, ffn_w2):
    bp, lp, _ = x_prompt.shape
    bs, ls, _ = x_sample.shape
    n_pages = page_table.shape[1]
    past_len = n_pages * cache_k.shape[2]
    pos_p = jnp.arange(lp)
    pos_s = past_len + jnp.arange(ls)
    xp, xs = x_prompt, x_sample
    kp, vp, fp, srp, sip, plp = [], [], [], [], [], []
    ksm, vsm, fsm, srs, sis, pls = [], [], [], [], [], []
    for layer in range(DEPTH):
        i = layer // 2
        hp = rmsnorm(xp, norm_mix_w[layer])
        hs = rmsnorm(xs, norm_mix_w[layer])
        if layer % 2 == 0:
            ssm_params = (s5_lam_re[i], s5_lam_im[i], s5_log_dt[i], s5_b_re[i], s5_b_im[i],
                          s5_c_re[i], s5_c_im[i], s5_d[i], w_glu[i], b_glu[i])
            q, k, v, logf, u = mixer_ab_projection(hp, w_in[i], b_f[i], q_norm_w[i], k_norm_w[i])
            att = fox_attend_prompt(q, k, v, logf)
            h0 = jnp.zeros((bp, N_SSM_GROUPS, SSM_STATE), F32)
            ssm, h_re, h_im = s5_mixer(u, h0, h0, *ssm_params)
            xp = xp + jnp.concatenate([att, ssm], axis=-1) @ w_out[i]
            kp.append(k); vp.append(v); fp.append(logf); srp.append(h_re); sip.append(h_im)
            q, k, v, logf, u = mixer_ab_projection(hs, w_in[i], b_f[i], q_norm_w[i], k_norm_w[i])
            k_past = cache_k[i, page_table].reshape(bs, past_len, N_HEADS, HEAD_DIM)
            v_past = cache_v[i, page_table].reshape(bs, past_len, N_HEADS, HEAD_DIM)
            f_past = cache_logf[i, page_table].reshape(bs, past_len, N_HEADS)
            att = fox_attend_sample(q, k, v, logf, k_past.astype(k.dtype), v_past.astype(v.dtype), f_past)
            ssm, h_re, h_im = s5_mixer(u, state_s5_re[i], state_s5_im[i], *ssm_params)
            xs = xs + jnp.concatenate([att, ssm], axis=-1) @ w_out[i]
            ksm.append(k); vsm.append(v); fsm.append(logf); srs.append(h_re); sis.append(h_im)
        else:
            ext_p = jnp.concatenate([jnp.zeros((bp, POOL_BUF, D_MODEL), hp.dtype), hp], axis=1)
            xp = xp + pool_mixer(ext_p, pos_p, pool_w[i], pool_scale[i])
            plp.append(ext_p[:, -POOL_BUF:])
            ext_s = jnp.concatenate([state_pool[i].astype(hs.dtype), hs], axis=1)
            xs = xs + pool_mixer(ext_s, pos_s, pool_w[i], pool_scale[i])
            pls.append(ext_s[:, -POOL_BUF:])
        xp = xp + swiglu(rmsnorm(xp, norm_ffn_w[layer]), ffn_w1[layer], ffn_w3[layer], ffn_w2[layer])
        xs = xs + swiglu(rmsnorm(xs, norm_ffn_w[layer]), ffn_w1[layer], ffn_w3[layer], ffn_w2[layer])
    return (xp, xs,
            jnp.stack(kp), jnp.stack(vp), jnp.stack(fp), jnp.stack(srp), jnp.stack(sip), jnp.stack(plp),
            jnp.stack(ksm), jnp.stack(vsm), jnp.stack(fsm), jnp.stack(srs), jnp.stack(sis), jnp.stack(pls))
```

```python
import numpy as np
from contextlib import ExitStack
import concourse.bass as bass
import concourse.mybir as mybir
from concourse.bass_utils import run_bass_kernel_spmd

F32 = mybir.dt.float32
BF16 = mybir.dt.bfloat16
I32 = mybir.dt.int32
AF = mybir.ActivationFunctionType
ALU = mybir.AluOpType
AX = mybir.AxisListType

D = 2048
DIN = 4104
DFF = 5632
NH = 8
HD = 128
NCORE = 8
EPS = 1e-6
SAME_ENGINE_SYNC = True


class TT:
    __slots__ = ("w", "r", "name", "excl")

    def __init__(self, name="", excl=False):
        self.w = None
        self.r = {}
        self.name = name
        self.excl = excl


class _Rec:
    def __init__(self):
        self.call = None

    def __getattr__(self, name):
        def f(*a, **k):
            self.call = (name, a, k)
        return f


class Prog:
    ENG = ("pe", "act", "dve", "pool", "sp")
    NDS = 8

    def __init__(self, nc):
        self.nc = nc
        self.ops = {e: [] for e in self.ENG}
        self.cnt = {e: 0 for e in self.ENG}
        self.waited = {e: {} for e in self.ENG}
        self.dma_n = {e: 0 for e in self.ENG}
        self.dma_val = {}

    def _deps(self, reads, writes):
        deps = {}

        def add(d):
            if d is None:
                return
            s, v = d
            if deps.get(s, 0) < v:
                deps[s] = v
        for t in reads:
            add(t.w)
            if t.excl:
                for s, v in t.r.items():
                    add((s, v))
        for t in writes:
            add(t.w)
            for s, v in t.r.items():
                add((s, v))
        return deps

    def _waits(self, eng, deps):
        wd = self.waited[eng]
        waits = []
        for s, v in deps.items():
            if wd.get(s, 0) < v:
                wd[s] = v
                waits.append((s, v))
        return waits

    def _mark(self, me, reads, writes):
        for t in writes:
            t.w = me
            t.r = {}
        for t in reads:
            if t.r.get(me[0], 0) < me[1]:
                t.r[me[0]] = me[1]

    def op(self, eng, fn, reads=(), writes=()):
        deps = self._deps(reads, writes)
        sem = "E_" + eng
        self.cnt[eng] += 1
        me = (sem, self.cnt[eng])
        if eng == "pe" or not SAME_ENGINE_SYNC:
            deps.pop(sem, None)
        rec = _Rec()
        fn(rec)
        name, a, k = rec.call
        fn2 = (lambda e, name=name, a=a, k=k: getattr(e, name)(*a, **k))
        self.ops[eng].append((self._waits(eng, deps), fn2, ("c", sem)))
        self._mark(me, reads, writes)

    def dma(self, q, out_ap, in_ap, reads=(), writes=(), **kw):
        deps = self._deps(reads, writes)
        n = self.dma_n[q]
        self.dma_n[q] += 1
        sem = "D_%s_%d" % (q, n % self.NDS)
        prev = self.dma_val.get(sem, 0)
        if prev:
            deps[sem] = max(deps.get(sem, 0), prev)
        tgt = prev + 16
        self.dma_val[sem] = tgt
        me = (sem, tgt)
        fn = (lambda e, o=out_ap, i=in_ap, k=kw: e.dma_start(out=o, in_=i, **k))
        self.ops[q].append((self._waits(q, deps), fn, ("d", sem)))
        self._mark(me, reads, writes)

    def dma_fn(self, q, fn, reads=(), writes=()):
        deps = self._deps(reads, writes)
        n = self.dma_n[q]
        self.dma_n[q] += 1
        sem = "D_%s_%d" % (q, n % self.NDS)
        prev = self.dma_val.get(sem, 0)
        if prev:
            deps[sem] = max(deps.get(sem, 0), prev)
        tgt = prev + 16
        self.dma_val[sem] = tgt
        rec = _Rec()
        fn(rec)
        name, a, k = rec.call
        fn2 = (lambda e, name=name, a=a, k=k: getattr(e, name)(*a, **k))
        self.ops[q].append((self._waits(q, deps), fn2, ("d", sem)))
        self._mark((sem, tgt), reads, writes)

    def barrier(self):
        allv = {"E_" + e: c for e, c in self.cnt.items() if c}
        allv.update(self.dma_val)
        for e in self.ENG:
            w = self._waits(e, dict(allv))
            if w:
                self.ops[e].append((w, None, None))

    def emit(self):
        nc = self.nc
        names = set(["E_" + e for e in self.ENG]) | set(self.dma_val.keys())
        with ExitStack() as es:
            sems = {n: es.enter_context(nc.semaphore(n)) for n in sorted(names)}
            block = es.enter_context(nc.Block())

            def mk(key):
                def body(e):
                    for waits, fn, sig in self.ops[key]:
                        for s, v in waits:
                            e.wait_ge(sems[s], v)
                        if fn is None:
                            continue
                        try:
                            ins = fn(e)
                        except Exception:
                            print("EMIT FAILURE on", key, getattr(fn, "__defaults__", None))
                            raise
                        if sig[0] == "d":
                            ins.then_inc(sems[sig[1]], 16)
                        else:
                            ins.then_inc(sems[sig[1]], 1)
                return body
            block.tensor(mk("pe"))
            block.scalar(mk("act"))
            block.vector(mk("dve"))
            block.gpsimd(mk("pool"))
            block.sync(mk("sp"))


class Arena:
    def __init__(self, nc, lo, hi):
        self.nc = nc
        self.lo = lo
        self.hi = hi
        self.top = lo
        self.n = 0

    def alloc(self, shape, dt, name="t"):
        esz = 4 if dt in (F32, I32) else 2
        nbytes = int(np.prod(shape[1:])) * esz
        off = (self.top + 63) // 64 * 64
        assert off + nbytes <= self.hi, "SBUF arena overflow %s %d" % (name, off + nbytes - self.hi)
        self.top = off + nbytes
        self.n += 1
        return self.nc.alloc_sbuf_tensor_at("%s_%d" % (name, self.n), list(shape), dt, offset=off)

    def alloc_top(self, shape, dt, name="t"):
        esz = 4 if dt in (F32, I32) else 2
        nbytes = int(np.prod(shape[1:])) * esz
        off = (self.hi - nbytes) // 64 * 64
        assert off >= self.top, "SBUF arena overflow (top) %s" % name
        self.hi = off
        self.n += 1
        return self.nc.alloc_sbuf_tensor_at("%s_%d" % (name, self.n), list(shape), dt, offset=off)

    def mark(self):
        return self.top

    def release(self, m):
        self.top = m


def build_program(cfg):
    nc = bass.Bass("TRN2", target_bir_lowering=False)
    P = Prog(nc)
    di = {}
    do = {}

    def inp(name, shape, dt=F32):
        di[name] = nc.dram_tensor(name, list(shape), dt, kind="ExternalInput").ap()
        return di[name]

    def outp(name, shape, dt=F32):
        do[name] = nc.dram_tensor(name, list(shape), dt, kind="ExternalOutput").ap()
        return do[name]

    NT = 16
    ACT0 = 7
    NA = NT - ACT0
    NTOK = NA * 128 + 64

    xw = inp("xw", [NT * 128, D])
    xs = inp("xs", [64, D])
    w_in = inp("w_in", [D, DIN])
    nmw = inp("norm_mix_w", [2, D])
    nfw = inp("norm_ffn_w", [2, D])
    b_f = inp("b_f", [1, NH])
    qnw = inp("q_norm_w", [1, HD])
    knw = inp("k_norm_w", [1, HD])
    cst = inp("cst", [128, 512])

    o_k = outp("o_k", [1024, NH * HD])
    o_v = outp("o_v", [1024, NH * HD])
    o_lf = outp("o_lf", [1024, NH])
    o_ks = outp("o_ks", [64, NH * HD])
    o_vs = outp("o_vs", [64, NH * HD])
    o_lfs = outp("o_lfs", [64, NH])

    A = Arena(nc, 20480, nc.SBUF_PARTITION_SIZE_BYTES - 1024)

    psum = [nc.alloc_psum_tensor("ps%d" % i, [128, 512], F32) for i in range(8)]
    pst = [TT("ps%d" % i, excl=True) for i in range(8)]
    pcur = [0]

    prot = [list(range(8))]

    def next_ps():
        lst = prot[0]
        i = lst[pcur[0] % len(lst)]
        pcur[0] += 1
        return psum[i], pst[i]

    def bf_view(ps):
        return ps[:, :].bitcast(BF16)

    cst_sb = A.alloc([128, 512], F32, "cst")
    t_cst = TT("cst")
    P.dma("sp", cst_sb[:, :], cst[:, :], writes=[t_cst])
    ident_f = cst_sb[:, 0:128]
    cst2 = inp("cst2", [128, 128])
    cst2_sb = A.alloc([128, 128], F32, "cst2")
    P.dma("sp", cst2_sb[:, :], cst2[:, :], writes=[t_cst])
    ones_f = A.alloc([128, 128], F32, "ones_f")
    ident_b = A.alloc([128, 128], BF16, "identb")
    t_identb = TT("identb")
    P.op("dve", lambda e: e.tensor_copy(out=ident_b[:, :], in_=cst_sb[:, 0:128]), reads=[t_cst], writes=[t_identb])

    eps_ref = [None]
    eps_sb = A.alloc([128, 1], F32, "eps")
    one_sb = A.alloc([128, 1], F32, "one")
    eps_ref[0] = eps_sb
    consts_top = A.top

    def bcast_load(name, src_ap, n):
        t = A.alloc([128, n], F32, name)
        tt = TT(name)
        P.dma("sp", t[:, :], src_ap.partition_broadcast(128), writes=[tt])
        return t, tt

    nmw0, t_nmw0 = bcast_load("nmw0", nmw[0:1, :], D)
    bf_sb, t_bf = bcast_load("bf", b_f[0:1, :], NH)
    qnw_sb, t_qnw = bcast_load("qnw", qnw[0:1, :], HD)
    knw_sb, t_knw = bcast_load("knw", knw[0:1, :], HD)

    def rmsnorm_tile(x_sb, t_x, rows, wbc, t_w, out_bf, t_out, scratch, t_scr, small, t_small):
        P.op("act", lambda e: e.activation(out=scratch[0:rows, :], in_=x_sb[0:rows, :], func=AF.Square,
                                           accum_out=small[0:rows, 0:1]),
             reads=[t_x], writes=[t_scr, t_small])
        P.op("act", lambda e: e.activation(out=small[0:rows, 1:2], in_=small[0:rows, 0:1], func=AF.Sqrt,
                                           scale=1.0 / D, bias=eps_ref[0][0:rows, 0:1]),
             reads=[t_small, t_cst2], writes=[t_small])
        P.op("dve", lambda e: e.reciprocal(out=small[0:rows, 2:3], in_=small[0:rows, 1:2]),
             reads=[t_small], writes=[t_small])
        P.op("dve", lambda e: e.scalar_tensor_tensor(out=out_bf[0:rows, :], in0=x_sb[0:rows, :],
                                                     scalar=small[0:rows, 2:3], in1=wbc[0:rows, :],
                                                     op0=ALU.mult, op1=ALU.mult),
             reads=[t_x, t_small, t_w], writes=[t_out])

    t_cst2 = TT("eps")
    P.op("pool", lambda e: e.memset(eps_sb[:, :], EPS), writes=[t_cst2])
    P.op("pool", lambda e: e.memset(one_sb[:, :], 1.0), writes=[t_cst2])
    P.op("pool", lambda e: e.memset(ones_f[:, :], 1.0), writes=[t_cst2])

    def transpose_to(hT, t_hT, col0, src_bf, t_src, rows, nchunk=16):
        for g in range(nchunk // 4):
            ps, tps = next_ps()
            psb = ps.ap().bitcast(BF16) if hasattr(ps, "ap") else ps[:, :].bitcast(BF16)
            for j in range(4):
                kc = g * 4 + j
                P.op("pe", lambda e, kc=kc, j=j, psb=psb: e.transpose(
                    out=psb[:, j * 128:j * 128 + rows], in_=src_bf[0:rows, kc * 128:(kc + 1) * 128],
                    identity=ident_b[0:rows, 0:rows]),
                    reads=[t_src, t_identb], writes=[tps])
            eng = "act" if g % 2 == 0 else "dve"
            if eng == "act":
                P.op("act", lambda e, g=g, psb=psb: e.copy(
                    out=hT[:, g * 4:(g + 1) * 4, col0:col0 + rows],
                    in_=psb[:, 0:512].rearrange("p (j t) -> p j t", j=4)[:, :, 0:rows]),
                    reads=[tps], writes=[t_hT])
            else:
                P.op("dve", lambda e, g=g, psb=psb: e.tensor_copy(
                    out=hT[:, g * 4:(g + 1) * 4, col0:col0 + rows],
                    in_=psb[:, 0:512].rearrange("p (j t) -> p j t", j=4)[:, :, 0:rows]),
                    reads=[tps], writes=[t_hT])

    SCALE = float(HD) ** -0.5
    t_attT = [TT("attT%d" % i) for i in range(NA + 1)]
    m0 = A.mark()
    ug_d = outp("ug_d", [128, 64, 272], BF16)
    t_ugd = TT("ug_d")

    def run_pass(SECT):
        m1 = A.mark()
        hT = A.alloc([128, 16, 1088], BF16, "hT")
        xbuf_off = (A.top + 63) // 64 * 64
        if 'u' in SECT:
            xbuf = [A.alloc([128, D], F32, "xb%d" % i) for i in range(2)]
            t_xbuf = [TT("xb%d" % i) for i in range(2)]
            hbf = [A.alloc([128, D], BF16, "hbf%d" % i) for i in range(2)]
            t_hbf = [TT("hbf%d" % i) for i in range(2)]
        else:
            xbuf = [A.alloc([128, D], F32, "xb%d" % i) for i in range(1)] * 2
            t_xbuf = [TT("xb%d" % i) for i in range(1)] * 2
            hbf = [A.alloc([128, D], BF16, "hbf%d" % i) for i in range(1)] * 2
            t_hbf = [TT("hbf%d" % i) for i in range(1)] * 2
        if 'u' in SECT:
            ugst = [A.alloc([128, 8, 128], BF16, "ugst%d" % i) for i in range(2)]
        t_ugst = [TT("ugst%d" % i) for i in range(2)]
        scr = A.alloc([128, 512], BF16, "scr")
        t_scr = TT("scr")
        small = [A.alloc([128, 16], F32, "small%d" % i) for i in range(4)]
        t_small = [TT("small%d" % i) for i in range(4)]
        CB = 512
        NHB = CB // 128
        wst = [A.alloc([128, 4, CB], F32, "wst%d" % i) for i in range(2)]
        t_wst = [TT("wst%d" % i) for i in range(2)]
        wstc = [0]
        wbf = [A.alloc([128, 16, CB], BF16, "wbf%d" % i) for i in range(2)]
        t_wbf = [TT("wbf%d" % i) for i in range(2)]
        ev32 = [A.alloc([128, CB], F32, "ev32_%d" % i) for i in range(2)] + [None]
        t_ev32 = [TT("ev32_%d" % i) for i in range(2)]
        evbf = [A.alloc([128, CB], BF16, "evbf_%d" % i) for i in range(2)]
        t_evbf = [TT("evbf_%d" % i) for i in range(2)]
        if 'u' in SECT:
            utok = A.alloc([128, 64, 128], BF16, "utok")
            utoks = A.alloc([16, 64, 128], BF16, "utoks")
        t_utok = TT("utok")
        t_utoks = TT("utoks")
        w_in_v = w_in.rearrange("(kc p) c -> p kc c", p=128)
        wcnt = [0]
        evc = [0]

        def load_wblock(c0, ncol):
            b = wcnt[0] % 2
            wcnt[0] += 1
            for q4 in range(4):
                sb = wstc[0] % 2
                wstc[0] += 1
                P.dma("sp", wst[sb][:, :, 0:ncol], w_in_v[:, q4 * 4:(q4 + 1) * 4, c0:c0 + ncol], writes=[t_wst[sb]])
                P.op("act", lambda e, b=b, sb=sb, q4=q4: e.copy(out=wbf[b][:, q4 * 4:(q4 + 1) * 4, 0:ncol], in_=wst[sb][:, :, 0:ncol]),
                     reads=[t_wst[sb]], writes=[t_wbf[b]])
            return wbf[b], t_wbf[b]

        t_hTs = [TT("hTs%d" % i) for i in range(9)]
        for hf in range(2):
            tiles = list(range(8)) if hf == 0 else list(range(8, 17))
            t_hTl = {lt: t_hTs[n] for n, lt in enumerate(tiles)}
            for n, lt in enumerate(tiles):
                b = n % 2
                rows = 128 if lt < 16 else 64
                src = xw[lt * 128:(lt + 1) * 128, :] if lt < 16 else xs[:, :]
                P.dma("sp", xbuf[b][0:rows, :], src, writes=[t_xbuf[b]])
                rmsnorm_tile(xbuf[b], t_xbuf[b], rows, nmw0, t_nmw0, hbf[b], t_hbf[b], hbf[b], t_hbf[b], small[b], t_small[b])
                transpose_to(hT, t_hTl[lt], n * 128, hbf[b], t_hbf[b], rows)

            def proj(lt, n, wb, t_wb, ncol):
                rows = 128 if lt < 16 else 64
                ps, tps = next_ps()
                for kc in range(16):
                    P.op("pe", lambda e, kc=kc: e.matmul(ps[0:rows, 0:ncol], lhsT=hT[:, kc, n * 128:n * 128 + rows],
                                                          rhs=wb[:, kc, 0:ncol], start=(kc == 0), stop=(kc == 15)),
                         reads=[t_hTl[lt], t_wb], writes=[tps])
                return ps, tps, rows

            def headnorm(ps, tps, rows, gain, t_gain, out32, t_out32, outbf, t_outbf):
                sb = evc[0] % 4
                sm, t_sm = small[sb], t_small[sb]
                for h2 in range(NHB):
                    P.op("act", lambda e, h2=h2: e.activation(out=scr[0:rows, h2 * 128:(h2 + 1) * 128],
                                                              in_=ps[0:rows, h2 * 128:(h2 + 1) * 128], func=AF.Square,
                                                              accum_out=sm[0:rows, h2:h2 + 1]),
                         reads=[tps], writes=[t_scr, t_sm])
                P.op("act", lambda e: e.activation(out=sm[0:rows, 4:4 + NHB], in_=sm[0:rows, 0:NHB], func=AF.Sqrt,
                                                   scale=1.0 / HD, bias=eps_sb[0:rows, 0:1]),
                     reads=[t_sm, t_cst2], writes=[t_sm])
                P.op("dve", lambda e: e.reciprocal(out=sm[0:rows, 8:8 + NHB], in_=sm[0:rows, 4:4 + NHB]), reads=[t_sm], writes=[t_sm])
                for h2 in range(NHB):
                    dst = out32 if out32 is not None else outbf
                    t_dst = t_out32 if out32 is not None else t_outbf
                    P.op("dve", lambda e, h2=h2, dst=dst: e.scalar_tensor_tensor(
                        out=dst[0:rows, h2 * 128:(h2 + 1) * 128], in0=ps[0:rows, h2 * 128:(h2 + 1) * 128],
                        scalar=sm[0:rows, 8 + h2:9 + h2], in1=gain[0:rows, :], op0=ALU.mult, op1=ALU.mult),
                        reads=[tps, t_sm, t_gain], writes=[t_dst])
                if out32 is not None:
                    P.op("act", lambda e: e.copy(out=outbf[0:rows, 0:CB], in_=out32[0:rows, 0:CB]),
                         reads=[t_out32], writes=[t_outbf])

            def tr2(src_bf, t_src, rows, dstT, t_dst, h0, col0):
                ps, tps = next_ps()
                psb = bf_view(ps)
                for h2 in range(NHB):
                    P.op("pe", lambda e, h2=h2: e.transpose(out=psb[:, h2 * 128:h2 * 128 + rows],
                                                            in_=src_bf[0:rows, h2 * 128:(h2 + 1) * 128],
                                                            identity=ident_b[0:rows, 0:rows]),
                         reads=[t_src, t_identb], writes=[tps])
                P.op("dve", lambda e: e.tensor_copy(out=dstT[:, h0:h0 + NHB, col0:col0 + rows],
                                                    in_=psb[:, 0:CB].rearrange("p (j t) -> p j t", j=NHB)[:, :, 0:rows]),
                     reads=[tps], writes=[t_dst])

            pend = []

            def tr2_delayed(args):
                pend.append(args)
                if len(pend) > 1:
                    tr2(*pend.pop(0))

            def tr2_flush():
                while pend:
                    tr2(*pend.pop(0))

            for cb in (range(1024 // CB) if 'q' in SECT else []):
                act_tiles = [(n, lt) for n, lt in enumerate(tiles) if lt >= ACT0]
                if not act_tiles:
                    continue
                wb, t_wb = load_wblock(cb * CB, CB)
                for n, lt in act_tiles:
                    ps, tps, rows = proj(lt, n, wb, t_wb, CB)
                    e3 = evc[0] % 2
                    evc[0] += 1
                    headnorm(ps, tps, rows, qg, t_qg, None, None, evbf[e3], t_evbf[e3])
                    ai = lt - ACT0
                    tr2_delayed((evbf[e3], t_evbf[e3], rows, QT, t_QT[ai], cb * NHB, ai * 128))
            for cb in (range(1024 // CB) if 'k' in SECT else []):
                wb, t_wb = load_wblock(1024 + cb * CB, CB)
                for n, lt in enumerate(tiles):
                    ps, tps, rows = proj(lt, n, wb, t_wb, CB)
                    e3 = evc[0] % 2
                    evc[0] += 1
                    headnorm(ps, tps, rows, knw_sb, t_knw, ev32[e3], t_ev32[e3], evbf[e3], t_evbf[e3])
                    if lt >= 8:
                        dst = o_k[(lt - 8) * 128:(lt - 7) * 128, cb * CB:(cb + 1) * CB] if lt < 16 else o_ks[:, cb * CB:(cb + 1) * CB]
                        P.dma("sp", dst, ev32[e3][0:rows, 0:CB], reads=[t_ev32[e3]])
                    tr2_delayed((evbf[e3], t_evbf[e3], rows, KT, t_KT[lt], cb * NHB, lt * 128))
            tr2_flush()
            for cb in (range(1024 // CB) if 'v' in SECT else []):
                wb, t_wb = load_wblock(2048 + cb * CB, CB)
                for n, lt in enumerate(tiles):
                    ps, tps, rows = proj(lt, n, wb, t_wb, CB)
                    P.op("act", lambda e, lt=lt, rows=rows, ps=ps: e.copy(out=Vb[0:rows, lt, cb * NHB:(cb + 1) * NHB, 0:HD], in_=ps[0:rows, 0:CB].rearrange("p (h d) -> p h d", h=NHB)),
                         reads=[tps], writes=[t_Vb[lt]])
                    if lt >= 8:
                        e3 = evc[0] % 2
                        evc[0] += 1
                        P.op("dve", lambda e, rows=rows, ps=ps, e3=e3: e.tensor_copy(out=ev32[e3][0:rows, 0:CB], in_=ps[0:rows, 0:CB]),
                             reads=[tps], writes=[t_ev32[e3]])
                        dst = o_v[(lt - 8) * 128:(lt - 7) * 128, cb * CB:(cb + 1) * CB] if lt < 16 else o_vs[:, cb * CB:(cb + 1) * CB]
                        P.dma("sp", dst, ev32[e3][0:rows, 0:CB], reads=[t_ev32[e3]])
            if 'f' in SECT:
                wb, t_wb = load_wblock(3072, NH)
            for n, lt in (enumerate(tiles) if 'f' in SECT else []):
                ps, tps, rows = proj(lt, n, wb, t_wb, NH)
                sb = evc[0] % 4
                evc[0] += 1
                sm, t_sm = small[sb], t_small[sb]
                P.op("dve", lambda e, rows=rows, ps=ps, sm=sm: e.tensor_tensor(out=sm[0:rows, 0:8], in0=ps[0:rows, 0:NH], in1=bf_sb[0:rows, :], op=ALU.add),
                     reads=[tps, t_bf], writes=[t_sm])
                P.op("act", lambda e, rows=rows, sm=sm: e.activation(out=sm[0:rows, 0:8], in_=sm[0:rows, 0:8], func=AF.Exp, scale=-1.0),
                     reads=[t_sm], writes=[t_sm])
                P.op("act", lambda e, rows=rows, sm=sm: e.activation(out=sm[0:rows, 0:8], in_=sm[0:rows, 0:8], func=AF.Ln, bias=one_sb[0:rows, 0:1]),
                     reads=[t_sm, t_cst2], writes=[t_sm])
                P.op("dve", lambda e, rows=rows, sm=sm, lt=lt: e.tensor_scalar(out=LF[0:rows, lt, :], in0=sm[0:rows, 0:8], scalar1=-1.0, scalar2=None, op0=ALU.mult),
                     reads=[t_sm], writes=[t_LF[lt]])
                if lt >= 8:
                    dst = o_lf[(lt - 8) * 128:(lt - 7) * 128, :] if lt < 16 else o_lfs[:, :]
                    P.dma("sp", dst, LF[0:rows, lt, :], reads=[t_LF[lt]])
            for cb in (range(1024 // CB) if 'u' in SECT else []):
                wb, t_wb = load_wblock(3080 + cb * CB, CB)
                for s in range(8):
                    ps, tps = next_ps()
                    for kc in range(16):
                        P.op("pe", lambda e, kc=kc, s=s, ps=ps: e.matmul(
                            ps[:, 0:CB], lhsT=hT[:, kc, 0:1024].rearrange("p (j s) -> p s j", s=8)[:, s, :],
                            rhs=wb[:, kc, 0:CB], start=(kc == 0), stop=(kc == 15)),
                            reads=[t_hTl[lt] for lt in tiles if lt < 16] + [t_wb], writes=[tps])
                    eng = "act" if s % 2 == 0 else "dve"
                    if eng == "act":
                        P.op("act", lambda e, s=s, ps=ps: e.copy(out=utok[:, cb * (CB // 16):(cb + 1) * (CB // 16), s * 16:(s + 1) * 16],
                                                                 in_=ps[:, 0:CB].rearrange("p (g c) -> p g c", c=16)),
                             reads=[tps], writes=[t_utok])
                    else:
                        P.op("dve", lambda e, s=s, ps=ps: e.tensor_copy(out=utok[:, cb * (CB // 16):(cb + 1) * (CB // 16), s * 16:(s + 1) * 16],
                                                                        in_=ps[:, 0:CB].rearrange("p (g c) -> p g c", c=16)),
                             reads=[tps], writes=[t_utok])
                if hf == 1:
                    if cb == 0:
                        P.op("pool", lambda e: e.memset(utoks[0:16, :, :], 0.0), writes=[t_utoks])
                    for s in range(4):
                        ps, tps = next_ps()
                        for kc in range(16):
                            P.op("pe", lambda e, kc=kc, s=s, ps=ps: e.matmul(
                                ps[0:16, 0:CB], lhsT=hT[:, kc, 1024:1088].rearrange("p (j s) -> p s j", s=4)[:, s, :],
                                rhs=wb[:, kc, 0:CB], start=(kc == 0), stop=(kc == 15)),
                                reads=[t_hTl[16], t_wb], writes=[tps])
                        P.op("act", lambda e, s=s, ps=ps: e.copy(out=utoks[0:16, cb * (CB // 16):(cb + 1) * (CB // 16), (s + 4) * 16:(s + 5) * 16],
                                                                 in_=ps[0:16, 0:CB].rearrange("p (g c) -> p g c", c=16)),
                             reads=[tps], writes=[t_utoks])
            for g8 in (range(8) if 'u' in SECT else []):
                ps, tps = next_ps()
                psb = bf_view(ps)
                for gg in range(8):
                    g = g8 * 8 + gg
                    P.op("pe", lambda e, g=g, gg=gg, psb=psb: e.transpose(
                        out=psb[:, gg * 128:(gg + 1) * 128], in_=utok[:, g, :], identity=ident_b[:, :]),
                        reads=[t_utok, t_identb], writes=[tps])
                ub = g8 % 2
                P.op("act" if g8 % 2 == 0 else "dve",
                     (lambda e, ub=ub, psb=psb: e.copy(out=ugst[ub][:, :, :], in_=psb[:, 0:1024].rearrange("p (g j) -> p g j", g=8)))
                     if g8 % 2 == 0 else
                     (lambda e, ub=ub, psb=psb: e.tensor_copy(out=ugst[ub][:, :, :], in_=psb[:, 0:1024].rearrange("p (g j) -> p g j", g=8))),
                     reads=[tps], writes=[t_ugst[ub]])
                P.dma("sp", ug_d[:, g8 * 8:(g8 + 1) * 8, hf * 128:(hf + 1) * 128], ugst[ub][:, :, :], reads=[t_ugst[ub]], writes=[t_ugd])
            if hf == 1 and 'u' in SECT:
                for g8 in range(8):
                    ps, tps = next_ps()
                    psb = bf_view(ps)
                    for gg in range(8):
                        g = g8 * 8 + gg
                        P.op("pe", lambda e, g=g, gg=gg, psb=psb: e.transpose(
                            out=psb[:, gg * 16:(gg + 1) * 16], in_=utoks[0:16, g, :], identity=ident_b[0:16, 0:16]),
                            reads=[t_utoks, t_identb], writes=[tps])
                    ub = g8 % 2
                    P.op("act", lambda e, ub=ub, psb=psb: e.copy(out=ugst[ub][:, :, 0:16],
                                                                in_=psb[:, 0:128].rearrange("p (g j) -> p g j", g=8)),
                         reads=[tps], writes=[t_ugst[ub]])
                    P.dma("sp", ug_d[:, g8 * 8:(g8 + 1) * 8, 256:272], ugst[ub][:, :, 0:16], reads=[t_ugst[ub]], writes=[t_ugd])
        A.release(m1)
        P.barrier()

    SECT_ALL = cfg.get('sect', 'qkvfu')
    if 'u' in SECT_ALL:
        run_pass('u')
    KT = A.alloc([128, NH, 2048 + 64], BF16, "KT")
    t_KT = [TT("KT%d" % i) for i in range(17)]
    Vb = A.alloc([128, 17, NH, HD + 1], BF16, "Vb")
    t_Vb = [TT("Vb%d" % i) for i in range(17)]
    QT = A.alloc([128, NH, NTOK], BF16, "QT")
    t_QT = [TT("QT%d" % i) for i in range(NA + 1)]
    LF = A.alloc([128, 17, NH], F32, "LF")
    t_LF = [TT("LF%d" % i) for i in range(17)]
    for lt_ in range(17):
        P.op("pool", lambda e, lt_=lt_: e.memset(Vb[:, lt_, :, HD:HD + 1], 1.0), writes=[t_Vb[lt_]])
    qg = A.alloc([128, HD], F32, "qg")
    t_qg = TT("qg")
    P.op("dve", lambda e: e.tensor_scalar(out=qg[:, :], in0=qnw_sb[:, :], scalar1=SCALE, scalar2=None, op0=ALU.mult),
         reads=[t_qnw], writes=[t_qg])

    run_pass(''.join(ch for ch in SECT_ALL if ch != 'u'))
    attT = A.alloc_top([128, 8, NTOK], BF16, "attT")
    P.barrier()
    if cfg.get("attn", True):
        m2 = A.mark()
        kmask = inp("kmask", [1, 2048])
        tri_b = A.alloc([128, 128], BF16, "tri_b")
        t_tri = TT("tri")
        P.op("dve", lambda e: e.tensor_copy(out=tri_b[:, :], in_=cst_sb[:, 128:256]), reads=[t_cst], writes=[t_tri])
        AK = A.alloc([128, 3, 2048], BF16, "AK")
        AQ = A.alloc([128, 3, 2048], BF16, "AQ")
        rows_bf = A.alloc([8, 4, 2048], BF16, "rows_bf")
        m2s = A.mark()
        lfT = A.alloc([8, 2048], F32, "lfT")
        t_lfT = TT("lfT")
        for g in range(4):
            ps, tps = next_ps()
            for j in range(4):
                lt = g * 4 + j
                P.op("pe", lambda e, lt=lt, j=j, ps=ps: e.transpose(out=ps[0:8, j * 128:(j + 1) * 128], in_=LF[:, lt, :], identity=cst_sb[:, 0:128]),
                     reads=[t_LF[lt], t_cst], writes=[tps])
            P.op("dve", lambda e, g=g, ps=ps: e.tensor_copy(out=lfT[0:8, g * 512:(g + 1) * 512], in_=ps[0:8, 0:512]), reads=[tps], writes=[t_lfT])
        cT = A.alloc([8, 2048], F32, "cT")
        t_cT = TT("cT")
        km32 = A.alloc([1, 2048], F32, "km32")
        kmb = A.alloc([1, 2048], BF16, "kmb")
        t_km = TT("km")
        P.dma("sp", km32[:, :], kmask[:, :], writes=[t_km])
        P.op("dve", lambda e: e.tensor_copy(out=kmb[:, :], in_=km32[:, :]), reads=[t_km], writes=[t_km])
        P.op("dve", lambda e: e.tensor_tensor_scan(out=cT[:, :], data0=one_sb[0:8, 0:1].broadcast_to([8, 2048]), data1=lfT[:, :], initial=0.0,
                                                   op0=ALU.mult, op1=ALU.add),
             reads=[t_lfT, t_cT, t_cst2], writes=[t_cT])
        t_rows = TT("rows_bf")
        tmp32 = A.alloc([8, 2048], F32, "tmp32")
        P.op("dve", lambda e: e.tensor_copy(out=rows_bf[:, 0, :], in_=cT[:, :]), reads=[t_cT], writes=[t_rows])
        P.op("dve", lambda e: e.tensor_tensor(out=tmp32[:, :], in0=cT[:, :], in1=rows_bf[:, 0, :], op=ALU.subtract), reads=[t_cT, t_rows], writes=[t_lfT])
        P.op("dve", lambda e: e.tensor_copy(out=rows_bf[:, 1, :], in_=tmp32[:, :]), reads=[t_lfT], writes=[t_rows])
        P.op("dve", lambda e: e.tensor_scalar(out=rows_bf[:, 2:4, :], in0=rows_bf[:, 0:2, :], scalar1=-1.0, scalar2=None, op0=ALU.mult), reads=[t_rows], writes=[t_rows])
        t_AK = TT("AK")
        t_AQ = TT("AQ")
        P.op("pool", lambda e: e.memset(AK[:, :, :], 1.0), writes=[t_AK])
        P.op("pool", lambda e: e.memset(AQ[:, :, :], 1.0), writes=[t_AQ])
        for h in range(8):
            pb_, sl_ = 32 * (h % 3), h // 3
            P.dma("sp", AK[pb_:pb_ + 1, sl_, :], rows_bf[h:h + 1, 2, :], reads=[t_rows], writes=[t_AK])
            P.dma("sp", AK[pb_ + 1:pb_ + 2, sl_, :], rows_bf[h:h + 1, 3, :], reads=[t_rows], writes=[t_AK])
            P.dma("sp", AK[pb_ + 4:pb_ + 5, sl_, :], kmb[:, :], reads=[t_km], writes=[t_AK])
            P.dma("sp", AQ[pb_ + 2:pb_ + 3, sl_, :], rows_bf[h:h + 1, 0, :], reads=[t_rows], writes=[t_AQ])
            P.dma("sp", AQ[pb_ + 3:pb_ + 4, sl_, :], rows_bf[h:h + 1, 1, :], reads=[t_rows], writes=[t_AQ])
        A.release(m2s)
        P.barrier()

        PT = [A.alloc([128, 1152], BF16, "PT%d" % i) for i in range(2)]
        t_PT = [TT("PT%d" % i) for i in range(2)]
        att_tok = [A.alloc([128, NH * HD], BF16, "att_tok%d" % i) for i in range(NA)]
        t_atok = [TT("atok%d" % i) for i in range(NA)]
        rden = A.alloc([128, 16], F32, "rden")
        t_rden = TT("rden")
        prot[0] = [0, 1, 2, 3, 4]
        pcur[0] = 0
        ptc = 0
        for h in range(8):
            acc_started = [False, False, False]

            def st_S(kt):
                q0t = max(kt, ACT0)
                nq = 16 - q0t
                ncols = nq * 128
                pb = kt % 2
                for c0 in range(0, ncols, 512):
                    cw = min(512, ncols - c0)
                    ps, tps = next_ps()
                    qa = (q0t - ACT0) * 128 + c0
                    ql = q0t * 128 + c0
                    P.op("pe", lambda e: e.matmul(
                        ps[:, 0:cw], lhsT=KT[:, h, kt * 128:(kt + 1) * 128], rhs=QT[:, h, qa:qa + cw], start=True, stop=False),
                        reads=[t_KT[kt]] + t_QT[0:NA], writes=[tps])
                    P.op("pe", lambda e: e.matmul(
                        ps[:, 0:cw], lhsT=AK[32 * (h % 3):32 * (h % 3) + 5, h // 3, kt * 128:(kt + 1) * 128],
                        rhs=AQ[32 * (h % 3):32 * (h % 3) + 5, h // 3, ql:ql + cw], start=False, stop=True),
                        reads=[t_AK, t_AQ], writes=[tps])
                    P.op("act", lambda e: e.activation(out=PT[pb][:, c0:c0 + cw], in_=ps[:, 0:cw], func=AF.Exp),
                         reads=[tps], writes=[t_PT[pb]])
                if kt >= ACT0:
                    P.op("pool", lambda e: e.tensor_tensor(out=PT[pb][:, 0:128], in0=PT[pb][:, 0:128], in1=tri_b[:, :], op=ALU.mult),
                         reads=[t_PT[pb], t_tri], writes=[t_PT[pb]])

            def st_PV(kt):
                q0t = max(kt, ACT0)
                nq = 16 - q0t
                pb = kt % 2
                for qi in range(nq):
                    ai = q0t - ACT0 + qi
                    bank = 5 + ai // 3
                    col = (ai % 3) * 129
                    first = not acc_started[ai // 3]
                    acc_started[ai // 3] = True
                    P.op("pe", lambda e: e.matmul(
                        psum[bank][:, col:col + 129], lhsT=PT[pb][:, qi * 128:(qi + 1) * 128], rhs=Vb[:, kt, h, :],
                        start=first, stop=(kt == 15), skip_group_check=True),
                        reads=[t_PT[pb], t_Vb[kt]], writes=[pst[bank]])
            for it in range(17):
                if it < 16:
                    st_S(it)
                if it >= 1:
                    st_PV(it - 1)
            for bank in (5, 6, 7):
                P.op("dve", lambda e, bank=bank, h=h: e.reciprocal(
                    out=rden[:, 0:3], in_=psum[bank][:, 0:387].rearrange("p (a c) -> p a c", c=129)[:, :, 128]),
                    reads=[pst[bank]], writes=[t_rden])
                for a3 in range(3):
                    ai = (bank - 5) * 3 + a3
                    P.op("dve", lambda e, bank=bank, a3=a3, ai=ai, h=h: e.tensor_scalar(
                        out=att_tok[ai][:, h * 128:(h + 1) * 128], in0=psum[bank][:, a3 * 129:a3 * 129 + 128],
                        scalar1=rden[:, a3:a3 + 1], scalar2=None, op0=ALU.mult),
                        reads=[pst[bank], t_rden], writes=[t_atok[ai]])
        prot[0] = list(range(8))
        for ai in range(NA):
            transpose_to(attT, t_attT[ai], ai * 128, att_tok[ai], t_atok[ai], 128, nchunk=8)
        if cfg.get("dbg_att", False):
            o_att = outp("o_att", [128, NA, NH * HD], BF16)
            for ai in range(NA):
                P.dma("sp", o_att[:, ai, :], att_tok[ai][:, :], reads=[t_atok[ai]])
        A.release(m2)
        P.barrier()
        m2 = A.mark()
        if cfg.get("sattn", True):
            ckv = inp("ckv", [2560 * 128, 2 * NH * HD]); clf = inp("clf", [2560, 128 * NH])
            ptab = inp("ptab", [1, 256], I32)
            NB, NPG = 16, 16
            pt_i = A.alloc([128, 256], I32, "pt_i"); pt_f = A.alloc([128, 256], F32, "pt_f"); idx = A.alloc([128, 256], I32, "idx")
            pid_i = A.alloc([128, 1], I32, "pid_i"); pid_f = A.alloc([128, 1], F32, "pid_f"); t_idx = TT("idx")
            P.dma("sp", pt_i[:, :], ptab[0:1, :].partition_broadcast(128), writes=[t_idx])
            P.op("pool", lambda e: e.iota(out=pid_i[:, :], pattern=[[0, 1]], base=0, channel_multiplier=1), writes=[t_idx])
            P.op("dve", lambda e: e.tensor_copy(out=pid_f[:, :], in_=pid_i[:, :]), reads=[t_idx], writes=[t_idx])
            P.op("dve", lambda e: e.tensor_copy(out=pt_f[:, :], in_=pt_i[:, :]), reads=[t_idx], writes=[t_idx])
            P.op("dve", lambda e: e.tensor_scalar(out=pt_f[:, :], in0=pt_f[:, :], scalar1=128.0, scalar2=pid_f[:, 0:1], op0=ALU.mult, op1=ALU.add),
                 reads=[t_idx], writes=[t_idx])
            P.op("dve", lambda e: e.tensor_copy(out=idx[:, :], in_=pt_f[:, :]), reads=[t_idx], writes=[t_idx])
            pgi = A.alloc([128, 2], I32, "pgi"); t_pgi = TT("pgi")
            P.dma("sp", pgi[:, :], ptab[0:1, :].rearrange("o (h p) -> p (o h)", p=128), writes=[t_pgi], allow_slow_non_contiguous=True)
            sfx = A.alloc([128, 2048], F32, "sfx"); t_sfx = TT("sfx")
            m2l = A.mark()
            lg = A.alloc([128, 2, 1024], F32, "lg"); t_lg = TT("lg")
            for hf_ in range(2):
                P.dma_fn("pool", lambda e: e.indirect_dma_start(out=lg[:, hf_, :], out_offset=None, in_=clf[:, :],
                                                                 in_offset=bass.IndirectOffsetOnAxis(ap=pgi[:, hf_:hf_ + 1], axis=0)),
                         reads=[t_pgi], writes=[t_lg])
            Fp = A.alloc([128, 2048], F32, "Fp"); t_Fp = TT("Fp")
            Fv = Fp[:, :].rearrange("p (b h j) -> p b h j", b=NB, h=NH)
            for hf_ in range(2):
                for h in range(NH):
                    ps, tps = next_ps()
                    P.op("pe", lambda e: e.transpose(out=ps[:, 0:128], in_=lg[:, hf_, :].rearrange("p (t h) -> p h t", h=NH)[:, h, :], identity=cst_sb[:, 0:128]),
                         reads=[t_lg, t_cst], writes=[tps])
                    P.op("dve", lambda e: e.tensor_copy(out=Fv[:, hf_ * 8:(hf_ + 1) * 8, h, :], in_=ps[:, 0:128].rearrange("p (b j) -> p b j", b=8)),
                         reads=[tps], writes=[t_Fp])
            Sin = A.alloc([128, 2048], F32, "Sin"); Tot = A.alloc([128, 2048], F32, "Tot"); msk0 = A.alloc([128, 2048], F32, "msk0"); t_S = TT("Sin")
            for c4 in range(4):
                ps, tps = next_ps()
                P.op("pe", lambda e: e.matmul(ps[:, 0:512], lhsT=cst_sb[:, 384:512], rhs=Fp[:, c4 * 512:(c4 + 1) * 512], start=True, stop=True),
                     reads=[t_cst, t_Fp], writes=[tps])
                P.op("dve", lambda e: e.tensor_copy(out=Sin[:, c4 * 512:(c4 + 1) * 512], in_=ps[:, 0:512]), reads=[tps], writes=[t_S])
                ps, tps = next_ps()
                P.op("pe", lambda e: e.matmul(ps[:, 0:512], lhsT=ones_f[:, :], rhs=Fp[:, c4 * 512:(c4 + 1) * 512], start=True, stop=True),
                     reads=[t_cst2, t_Fp], writes=[tps])
                P.op("act", lambda e: e.copy(out=Tot[:, c4 * 512:(c4 + 1) * 512], in_=ps[:, 0:512]), reads=[tps], writes=[t_S])
            P.op("pool", lambda e: e.memset(msk0[:, :], 1.0), reads=[t_S], writes=[t_S])
            P.op("pool", lambda e: e.memset(msk0[:, :].rearrange("p (a j) -> p a j", j=NPG)[:, :, 0:1], 0.0), reads=[t_S], writes=[t_S])
            P.op("dve", lambda e: e.tensor_tensor_scan(out=Fp[:, :], data0=msk0[:, :], data1=Tot[:, :], initial=0.0, op0=ALU.mult, op1=ALU.add),
                 reads=[t_S, t_Fp], writes=[t_Fp])
            Pv = Fp[:, :].rearrange("p (a j) -> p a j", j=NPG)
            P.op("dve", lambda e: e.tensor_tensor(out=Tot[:, :].rearrange("p (a j) -> p a j", j=NPG), in0=Pv[:, :, NPG - 1:NPG].broadcast_to([128, 128, NPG]),
                                                  in1=Pv, op=ALU.subtract), reads=[t_Fp, t_S], writes=[t_S])
            P.op("dve", lambda e: e.tensor_tensor(out=sfx[:, :], in0=Sin[:, :], in1=Tot[:, :], op=ALU.add), reads=[t_S], writes=[t_sfx])
            P.barrier()
            A.release(m2l)
            sfxv = sfx[:, :].rearrange("p (b h j) -> p b h j", b=NB, h=NH)
            cumn = A.alloc([64, NH], F32, "cumn"); t_cumn = TT("cumn")
            ps, tps = next_ps()
            P.op("pe", lambda e: e.matmul(ps[0:64, 0:NH], lhsT=cst2_sb[0:64, 0:64], rhs=LF[0:64, 16, :], start=True, stop=True),
                 reads=[t_cst, t_LF[16]], writes=[tps])
            P.op("dve", lambda e: e.tensor_scalar(out=cumn[:, :], in0=ps[0:64, 0:NH], scalar1=-1.0, scalar2=None, op0=ALU.mult), reads=[tps], writes=[t_cumn])
            KVpg = [A.alloc([128, 2 * NH * HD], F32, "KVpg%d" % i) for i in range(3)]; t_KVpg = [TT("KVpg%d" % i) for i in range(3)]
            KTp = [A.alloc([128, NH, 128], BF16, "KTp%d" % i) for i in range(2)]; t_KTp = [TT("KTp%d" % i) for i in range(2)]
            Vpb = [A.alloc([128, NH, HD + 1], BF16, "Vpb%d" % i) for i in range(3)]; t_Vpb = [TT("Vpb%d" % i) for i in range(3)]
            PTp = [A.alloc([128, NH, 64], BF16, "PTp%d" % i) for i in range(3)]; t_PTp = [TT("PTp%d" % i) for i in range(3)]
            sct = [A.alloc([128, NH, 4], F32, "sct%d" % i) for i in range(2)]; t_sct = [TT("sct%d" % i) for i in range(2)]
            for i3 in range(3):
                P.op("pool", lambda e: e.memset(Vpb[i3][:, :, HD:HD + 1], 1.0), writes=[t_Vpb[i3]])
            prot[0] = [4]
            pcur[0] = 0
            sacc_started = [False, False, False]
            QS0 = NA * 128

            def pv_acc(lhsT_of_h, rhs_of_h, extra_reads, last):
                for h in range(NH):
                    bank = 5 + h // 3
                    col = (h % 3) * 129
                    first = not sacc_started[h // 3]
                    sacc_started[h // 3] = True
                    P.op("pe", lambda e: e.matmul(psum[bank][0:64, col:col + 129], lhsT=lhsT_of_h(h), rhs=rhs_of_h(h),
                                                  start=first, stop=last, skip_group_check=True),
                         reads=extra_reads, writes=[pst[bank]])
            Kb16 = [A.alloc([128, NH * HD], BF16, "Kb16_%d" % i) for i in range(3)]; t_Kb16 = [TT("Kb16_%d" % i) for i in range(3)]
            pages = [(b, j) for b in range(NB) for j in range(NPG)]
            NPAGE = len(pages)
            ptp_last = [None, None, None]

            def st_load(n):
                b, j = pages[n]
                k3 = n % 3
                col = b * NPG + j
                P.dma_fn("pool", lambda e: e.indirect_dma_start(out=KVpg[k3][:, :], out_offset=None, in_=ckv[:, :],
                                                                 in_offset=bass.IndirectOffsetOnAxis(ap=idx[:, col:col + 1], axis=0)),
                         reads=[t_idx], writes=[t_KVpg[k3]])
                P.op("act", lambda e: e.copy(out=Vpb[k3][:, :, 0:HD], in_=KVpg[k3][:, NH * HD:2 * NH * HD].rearrange("p (h d) -> p h d", h=NH)),
                     reads=[t_KVpg[k3]], writes=[t_Vpb[k3]])
                P.op("dve", lambda e: e.tensor_copy(out=Kb16[k3][:, :], in_=KVpg[k3][:, 0:NH * HD]), reads=[t_KVpg[k3]], writes=[t_Kb16[k3]])

            def st_T(n):
                k3, k2 = n % 3, n % 2
                bank = n % 2
                psb = bf_view(psum[bank])
                for h in range(NH):
                    P.op("pe", lambda e: e.transpose(out=psb[:, h * 128:(h + 1) * 128], in_=Kb16[k3][:, h * HD:(h + 1) * HD], identity=ident_b[:, :]),
                         reads=[t_Kb16[k3], t_identb], writes=[pst[bank]])
                if n % 2 == 0:
                    P.op("act", lambda e: e.copy(out=KTp[k2][:, :, :], in_=psb[:, 0:1024].rearrange("p (h t) -> p h t", h=NH)), reads=[pst[bank]], writes=[t_KTp[k2]])
                else:
                    P.op("dve", lambda e: e.tensor_copy(out=KTp[k2][:, :, :], in_=psb[:, 0:1024].rearrange("p (h t) -> p h t", h=NH)), reads=[pst[bank]], writes=[t_KTp[k2]])

            def st_S(n):
                b, j = pages[n]
                k3, k2 = n % 3, n % 2
                bank = 2 + n % 2
                ps = psum[bank]
                for h in range(NH):
                    P.op("pe", lambda e: e.matmul(ps[:, h * 4:(h + 1) * 4], lhsT=KTp[k2][:, h, :], rhs=QT[:, h, QS0 + b * 4:QS0 + b * 4 + 4], start=True, stop=True),
                         reads=[t_KTp[k2], t_QT[NA]], writes=[pst[bank]])
                P.op("dve", lambda e: e.tensor_tensor(out=sct[k2][:, :, :], in0=ps[:, 0:32].rearrange("p (h q) -> p h q", h=NH),
                                                      in1=sfxv[:, b, :, j:j + 1].broadcast_to([128, NH, 4]), op=ALU.add),
                     reads=[pst[bank], t_sfx], writes=[t_sct[k2]])
                if ptp_last[k3] is None:
                    P.op("pool", lambda e: e.memset(PTp[k3][:, :, :], 0.0), writes=[t_PTp[k3]])
                elif ptp_last[k3] != b:
                    pb_ = ptp_last[k3]
                    P.op("pool", lambda e: e.memset(PTp[k3][:, :, pb_ * 4:pb_ * 4 + 4], 0.0), writes=[t_PTp[k3]])
                ptp_last[k3] = b
                P.op("act", lambda e: e.activation(out=PTp[k3][:, :, b * 4:b * 4 + 4], in_=sct[k2][:, :, :], func=AF.Exp), reads=[t_sct[k2]], writes=[t_PTp[k3]])

            def st_PV(n):
                k3 = n % 3
                pv_acc(lambda h: PTp[k3][:, h, :], lambda h: Vpb[k3][:, h, :], [t_PTp[k3], t_Vpb[k3]], False)

            for it in range(NPAGE + 3):
                if 0 <= it - 3 < NPAGE:
                    st_PV(it - 3)
                if 0 <= it - 2 < NPAGE:
                    st_S(it - 2)
                if 0 <= it - 1 < NPAGE:
                    st_T(it - 1)
                if it < NPAGE:
                    st_load(it)
            PTn = A.alloc([64, NH, 64], BF16, "PTn"); t_PTn = TT("PTn")
            scn = A.alloc([64, NH, 64], F32, "scn"); t_scn = TT("scn")
            mskn = A.alloc([64, 64], BF16, "mskn")
            P.op("dve", lambda e: e.tensor_copy(out=mskn[:, :], in_=cst2_sb[0:64, 0:64]), reads=[t_cst], writes=[t_scn])
            ps, tps = next_ps()
            for h in range(NH):
                P.op("pe", lambda e: e.matmul(ps[0:64, h * 64:(h + 1) * 64], lhsT=KT[:, h, 2048:2112], rhs=QT[:, h, QS0:QS0 + 64], start=True, stop=True),
                     reads=[t_KT[16], t_QT[NA]], writes=[tps])
            P.op("dve", lambda e: e.tensor_tensor(out=scn[:, :, :], in0=ps[0:64, 0:512].rearrange("p (h q) -> p h q", h=NH),
                                                  in1=cumn[:, :].unsqueeze(2).broadcast_to([64, NH, 64]), op=ALU.add), reads=[tps, t_cumn], writes=[t_scn])
            P.op("act", lambda e: e.activation(out=scn[:, :, :], in_=scn[:, :, :], func=AF.Exp), reads=[t_scn], writes=[t_scn])
            P.op("dve", lambda e: e.tensor_tensor(out=PTn[:, :, :], in0=scn[:, :, :], in1=mskn[:, :].unsqueeze(1).broadcast_to([64, NH, 64]), op=ALU.mult),
                 reads=[t_scn], writes=[t_PTn])
            pv_acc(lambda h: PTn[:, h, :], lambda h: Vb[0:64, 16, h, :], [t_PTn, t_Vb[16]], True)
            atk_s = A.alloc([64, NH * HD], BF16, "atk_s"); t_atks = TT("atk_s")
            rdn = A.alloc([64, 8], F32, "rdn"); t_rdn = TT("rdn")
            for bank in (5, 6, 7):
                nh_ = 3 if bank < 7 else 2
                P.op("dve", lambda e: e.reciprocal(out=rdn[:, 0:nh_], in_=psum[bank][0:64, 0:nh_ * 129].rearrange("p (a c) -> p a c", c=129)[:, :, 128]),
                     reads=[pst[bank]], writes=[t_rdn])
                for a3 in range(nh_):
                    h = (bank - 5) * 3 + a3
                    P.op("dve", lambda e: e.tensor_scalar(out=atk_s[:, h * 128:(h + 1) * 128], in0=psum[bank][0:64, a3 * 129:a3 * 129 + 128],
                                                          scalar1=rdn[:, a3:a3 + 1], scalar2=None, op0=ALU.mult),
                         reads=[pst[bank], t_rdn], writes=[t_atks])
            prot[0] = list(range(8))
            transpose_to(attT, t_attT[NA], NA * 128, atk_s, t_atks, 64, nchunk=8)
        A.release(m2)
    A.release(m0)
    P.barrier()
    t_ssmT = TT("ssmT")
    Zg = A.alloc([128, 64, 160], BF16, "Zg"); t_Zg = TT("Zg")
    mS = A.mark()
    if cfg.get("s5", True):
        TWO_PI = 2.0 * np.pi
        lam_re = inp("s5_lam_re", [64, 64]); lam_im = inp("s5_lam_im", [64, 64]); log_dt = inp("s5_log_dt", [1, 64])
        b_re = inp("s5_b_re", [64, 64, 16]); b_im = inp("s5_b_im", [64, 64, 16])
        c_re = inp("s5_c_re", [1024, 64]); c_im = inp("s5_c_im", [1024, 64]); d_in = inp("s5_d", [64, 16])
        w_glu = inp("w_glu", [1024, 1024]); b_glu = inp("b_glu", [8, 128])
        st_re = inp("st_re", [1024, 64]); st_im = inp("st_im", [1024, 64])
        o_s5r = outp("o_s5r", [64, 64]); o_s5i = outp("o_s5i", [64, 64])
        o_s5rs = outp("o_s5rs", [1024, 64]); o_s5is = outp("o_s5is", [1024, 64])
        t_su = TT("s5setup")

        def tt(out, a, b, op, eng="dve", R=(), W=()):
            P.op(eng, lambda e: e.tensor_tensor(out=out, in0=a, in1=b, op=op), reads=[t_su] + list(R), writes=[t_su] + list(W))

        def ts(out, a, s1, op0, s2=None, op1=None, R=(), W=()):
            if op1 is None:
                P.op("dve", lambda e: e.tensor_scalar(out=out, in0=a, scalar1=s1, scalar2=None, op0=op0), reads=[t_su] + list(R), writes=[t_su] + list(W))
            else:
                P.op("dve", lambda e: e.tensor_scalar(out=out, in0=a, scalar1=s1, scalar2=s2, op0=op0, op1=op1), reads=[t_su] + list(R), writes=[t_su] + list(W))

        def actf(out, a, func, scale=1.0, R=(), W=()):
            P.op("act", lambda e: e.activation(out=out, in_=a, func=func, scale=scale), reads=[t_su] + list(R), writes=[t_su] + list(W))

        def cp(out, a, eng="dve", R=(), W=()):
            if eng == "act":
                P.op("act", lambda e: e.copy(out=out, in_=a), reads=[t_su] + list(R), writes=[t_su] + list(W))
            else:
                P.op(eng, lambda e: e.tensor_copy(out=out, in_=a), reads=[t_su] + list(R), writes=[t_su] + list(W))

        def trf(dst, src, rows, cols, R=(), W=(), eng="dve"):
            ps, tps = next_ps()
            P.op("pe", lambda e: e.transpose(out=ps[0:cols, 0:rows], in_=src, identity=cst_sb[0:rows, 0:rows]),
                 reads=[t_su, t_cst] + list(R), writes=[tps])
            P.op(eng, (lambda e: e.tensor_copy(out=dst, in_=ps[0:cols, 0:rows])) if eng != "act" else (lambda e: e.copy(out=dst, in_=ps[0:cols, 0:rows])),
                 reads=[tps], writes=[t_su] + list(W))

        Mg = A.alloc([128, 64, 128], BF16, "Mg")
        Wr = A.alloc([128, 64, 64], BF16, "Wr"); Wi = A.alloc([128, 64, 64], BF16, "Wi")
        Qst = A.alloc([128, 64, 128], BF16, "Qst")
        A8r = A.alloc([64, 64], F32, "A8r"); A8i = A.alloc([64, 64], F32, "A8i")
        A4r = A.alloc([64, 64], F32, "A4r"); A4i = A.alloc([64, 64], F32, "A4i")
        I4r = A.alloc([64, 64], F32, "I4r"); I4i = A.alloc([64, 64], F32, "I4i")
        m3 = A.mark()
        S = lambda n: A.alloc([64, 64], F32, n)
        ld = A.alloc([64, 64], F32, "ld")
        lr = S("lr"); li = S("li"); dtt = S("dt"); mag = S("mag"); ang = S("ang"); t1 = S("t1"); t2 = S("t2"); t3 = S("t3")
        sn = S("sn"); cs = S("cs"); ar = S("ar"); ai = S("ai"); zr = S("zr"); zi = S("zi"); iar = S("iar"); iai = S("iai")
        ki = A.alloc([64, 64], I32, "ki")
        P.dma("sp", ld[:, :], lam_re[:, :], writes=[t_su])
        trf(lr[:, :], ld[:, :], 64, 64)
        P.dma("sp", ld[:, :], lam_im[:, :], writes=[t_su], reads=[t_su])
        trf(li[:, :], ld[:, :], 64, 64)
        P.dma("sp", dtt[:, :], log_dt[0:1, :].partition_broadcast(64), writes=[t_su], reads=[t_su])
        actf(dtt[:, :], dtt[:, :], AF.Exp)
        tt(t1[:, :], lr[:, :], dtt[:, :], ALU.mult)
        actf(mag[:, :], t1[:, :], AF.Exp)
        tt(ang[:, :], li[:, :], dtt[:, :], ALU.mult)

        def sin_of(out, angle, shift):
            ts(t1[:, :], angle, 1.0 / TWO_PI, ALU.mult, shift / TWO_PI, ALU.add)
            cp(ki[:, :], t1[:, :])
            cp(t2[:, :], ki[:, :])
            ts(t2[:, :], t2[:, :], -TWO_PI, ALU.mult)
            ts(t1[:, :], angle, shift, ALU.add)
            tt(t1[:, :], t1[:, :], t2[:, :], ALU.add)
            ts(t2[:, :], t1[:, :], float(np.pi), ALU.is_gt, -TWO_PI, ALU.mult)
            tt(t1[:, :], t1[:, :], t2[:, :], ALU.add)
            ts(t2[:, :], t1[:, :], float(-np.pi), ALU.is_lt, TWO_PI, ALU.mult)
            tt(t1[:, :], t1[:, :], t2[:, :], ALU.add)
            actf(out, t1[:, :], AF.Sin)
        sin_of(sn[:, :], ang[:, :], 0.0)
        sin_of(cs[:, :], ang[:, :], float(np.pi / 2))
        tt(ar[:, :], mag[:, :], cs[:, :], ALU.mult)
        tt(ai[:, :], mag[:, :], sn[:, :], ALU.mult)
        am1 = S("am1"); den = S("den")
        ts(am1[:, :], ar[:, :], -1.0, ALU.add)
        tt(t1[:, :], lr[:, :], lr[:, :], ALU.mult); tt(t2[:, :], li[:, :], li[:, :], ALU.mult); tt(den[:, :], t1[:, :], t2[:, :], ALU.add)
        P.op("dve", lambda e: e.reciprocal(out=den[:, :], in_=den[:, :]), reads=[t_su], writes=[t_su])
        tt(t1[:, :], am1[:, :], lr[:, :], ALU.mult); tt(t2[:, :], ai[:, :], li[:, :], ALU.mult); tt(t1[:, :], t1[:, :], t2[:, :], ALU.add); tt(zr[:, :], t1[:, :], den[:, :], ALU.mult)
        tt(t1[:, :], ai[:, :], lr[:, :], ALU.mult); tt(t2[:, :], am1[:, :], li[:, :], ALU.mult); tt(t1[:, :], t1[:, :], t2[:, :], ALU.subtract); tt(zi[:, :], t1[:, :], den[:, :], ALU.mult)
        tt(t1[:, :], ar[:, :], ar[:, :], ALU.mult); tt(t2[:, :], ai[:, :], ai[:, :], ALU.mult); tt(t1[:, :], t1[:, :], t2[:, :], ALU.add)
        P.op("dve", lambda e: e.reciprocal(out=t1[:, :], in_=t1[:, :]), reads=[t_su], writes=[t_su])
        tt(iar[:, :], ar[:, :], t1[:, :], ALU.mult)
        tt(iai[:, :], ai[:, :], t1[:, :], ALU.mult); ts(iai[:, :], iai[:, :], -1.0, ALU.mult)
        pwr = A.alloc([64, 64, 9], F32, "pwr"); pwi = A.alloc([64, 64, 9], F32, "pwi")
        ipr = A.alloc([64, 64, 9], F32, "ipr"); ipi = A.alloc([64, 64, 9], F32, "ipi")
        rvr = A.alloc([64, 64, 8], F32, "rvr"); rvi = A.alloc([64, 64, 8], F32, "rvi")

        def powers(pr, pi_, br_, bi_, n):
            P.op("pool", lambda e: e.memset(pr[:, :, 0], 1.0), reads=[t_su], writes=[t_su])
            P.op("pool", lambda e: e.memset(pi_[:, :, 0], 0.0), reads=[t_su], writes=[t_su])
            cp(pr[:, :, 1], br_); cp(pi_[:, :, 1], bi_)
            for k in range(2, n + 1):
                tt(t1[:, :], pr[:, :, k - 1], br_, ALU.mult); tt(t2[:, :], pi_[:, :, k - 1], bi_, ALU.mult); tt(pr[:, :, k], t1[:, :], t2[:, :], ALU.subtract)
                tt(t1[:, :], pr[:, :, k - 1], bi_, ALU.mult); tt(t2[:, :], pi_[:, :, k - 1], br_, ALU.mult); tt(pi_[:, :, k], t1[:, :], t2[:, :], ALU.add)
        powers(pwr, pwi, ar[:, :], ai[:, :], 8)
        powers(ipr, ipi, iar[:, :], iai[:, :], 7)
        for s_ in range(8):
            cp(rvr[:, :, s_], pwr[:, :, 7 - s_]); cp(rvi[:, :, s_], pwi[:, :, 7 - s_])
        cp(A8r[:, :], pwr[:, :, 8]); cp(A8i[:, :], pwi[:, :, 8])
        cp(A4r[:, :], pwr[:, :, 4]); cp(A4i[:, :], pwi[:, :, 4])
        cp(I4r[:, :], ipr[:, :, 4]); cp(I4i[:, :], ipi[:, :, 4])
        brt = A.alloc([64, 64, 16], F32, "brt"); bit = A.alloc([64, 64, 16], F32, "bit")
        bbr = A.alloc([64, 64, 16], F32, "bbr"); bbi = A.alloc([64, 64, 16], F32, "bbi")
        tb1 = A.alloc([64, 64, 16], F32, "tb1")
        P.dma("sp", brt[:, :, :], b_re.rearrange("g p c -> p g c"), reads=[t_su], writes=[t_su])
        P.dma("sp", bit[:, :, :], b_im.rearrange("g p c -> p g c"), reads=[t_su], writes=[t_su])
        zrb = zr[:, :].unsqueeze(2).broadcast_to([64, 64, 16]); zib = zi[:, :].unsqueeze(2).broadcast_to([64, 64, 16])
        tt(bbr[:, :, :], brt[:, :, :], zrb, ALU.mult); tt(tb1[:, :, :], bit[:, :, :], zib, ALU.mult); tt(bbr[:, :, :], bbr[:, :, :], tb1[:, :, :], ALU.subtract)
        tt(bbi[:, :, :], bit[:, :, :], zrb, ALU.mult); tt(tb1[:, :, :], brt[:, :, :], zib, ALU.mult); tt(bbi[:, :, :], bbi[:, :, :], tb1[:, :, :], ALU.add)
        Cr = brt; Ci = bit
        cld = A.alloc([128, 64], F32, "cld")
        for src, dstC in ((c_re, Cr), (c_im, Ci)):
            for r8 in range(8):
                P.dma("sp", cld[:, :], src[r8 * 128:(r8 + 1) * 128, :], reads=[t_su], writes=[t_su])
                trf(dstC[:, r8 * 8:(r8 + 1) * 8, :].rearrange("p g c -> p (g c)"), cld[:, :], 128, 64)
        dld = A.alloc([64, 16], F32, "dld"); drep = A.alloc([64, 8, 16], F32, "drep"); dcol = A.alloc([128, 64], F32, "dcol")
        P.dma("sp", dld[:, :], d_in[:, :], reads=[t_su], writes=[t_su])
        cp(drep[:, :, :], dld[:, :].unsqueeze(1).broadcast_to([64, 8, 16]))
        trf(dcol[:, :], drep[:, :, :].rearrange("g s c -> g (s c)"), 64, 128)
        GB = 8
        CAr = A.alloc([64, GB, 9, 16], F32, "CAr"); CAi = A.alloc([64, GB, 9, 16], F32, "CAi"); nCAi = A.alloc([64, GB, 9, 16], F32, "nCAi")
        tq = A.alloc([64, GB, 9, 16], F32, "tq")
        Lr = A.alloc([64, GB, 8, 16], F32, "Lr"); Li = A.alloc([64, GB, 8, 16], F32, "Li")
        Wtr = A.alloc([64, GB, 8, 16], F32, "Wtr"); Wti = A.alloc([64, GB, 8, 16], F32, "Wti")
        mtmp2 = [A.alloc([128, 128], F32, "mtmp%d" % i) for i in range(2)]
        t_mt = [TT("mtmp%d" % i) for i in range(2)]; t_Mgw = [TT("Mgw%d" % i) for i in range(2)]; t_Ww = [TT("Ww%d" % i) for i in range(2)]
        t_gb = TT("gb")
        gbflag = A.alloc([128, 2], F32, "gbflag")
        P.op("pool", lambda e: e.memset(gbflag[:, :], 0.0), reads=[t_su], writes=[t_su])
        maskM = cst_sb[:, 256:384]
        for gb in range(64 // GB):
            gs = slice(gb * GB, (gb + 1) * GB)
            Cb_r = Cr[:, gs, :].unsqueeze(2).broadcast_to([64, GB, 9, 16]); Cb_i = Ci[:, gs, :].unsqueeze(2).broadcast_to([64, GB, 9, 16])
            Pb_r = pwr[:, gs, :].unsqueeze(3).broadcast_to([64, GB, 9, 16]); Pb_i = pwi[:, gs, :].unsqueeze(3).broadcast_to([64, GB, 9, 16])
            tt(CAr[:, :, :, :], Cb_r, Pb_r, ALU.mult); tt(tq[:, :, :, :], Cb_i, Pb_i, ALU.mult); tt(CAr[:, :, :, :], CAr[:, :, :, :], tq[:, :, :, :], ALU.subtract)
            tt(CAi[:, :, :, :], Cb_r, Pb_i, ALU.mult); tt(tq[:, :, :, :], Cb_i, Pb_r, ALU.mult); tt(CAi[:, :, :, :], CAi[:, :, :, :], tq[:, :, :, :], ALU.add)
            ts(nCAi[:, :, :, :], CAi[:, :, :, :], -1.0, ALU.mult)
            cp(Qst[0:64, gs, :].rearrange("p g (k c) -> p g k c", k=8), CAr[:, :, 1:9, :])
            cp(Qst[64:128, gs, :].rearrange("p g (k c) -> p g k c", k=8), nCAi[:, :, 1:9, :])
            Bb_r = bbr[:, gs, :].unsqueeze(2).broadcast_to([64, GB, 8, 16]); Bb_i = bbi[:, gs, :].unsqueeze(2).broadcast_to([64, GB, 8, 16])
            tq8 = tq[:, :, 0:8, :]
            for (Or, Oi, Xr_, Xi_) in ((Lr, Li, ipr, ipi), (Wtr, Wti, rvr, rvi)):
                Xb_r = Xr_[:, gs, 0:8].unsqueeze(3).broadcast_to([64, GB, 8, 16]); Xb_i = Xi_[:, gs, 0:8].unsqueeze(3).broadcast_to([64, GB, 8, 16])
                tt(Or[:, :, :, :], Bb_r, Xb_r, ALU.mult); tt(tq8, Bb_i, Xb_i, ALU.mult); tt(Or[:, :, :, :], Or[:, :, :, :], tq8, ALU.subtract)
                tt(Oi[:, :, :, :], Bb_r, Xb_i, ALU.mult); tt(tq8, Bb_i, Xb_r, ALU.mult); tt(Oi[:, :, :, :], Oi[:, :, :, :], tq8, ALU.add)
            P.op("dve", lambda e: e.tensor_copy(out=gbflag[:, 0:1], in_=gbflag[:, 1:2]), reads=[t_su], writes=[t_gb])
            for gl in range(GB):
                g = gb * GB + gl
                mb_ = g % 2
                ps, tps = next_ps()
                P.op("pe", lambda e: e.matmul(ps[:, 0:128], lhsT=Lr[:, gl, :, :].rearrange("p s c -> p (s c)"),
                                              rhs=CAr[:, gl, 0:8, :].rearrange("p k c -> p (k c)"), start=True, stop=False),
                     reads=[t_gb], writes=[tps])
                P.op("pe", lambda e: e.matmul(ps[:, 0:128], lhsT=Li[:, gl, :, :].rearrange("p s c -> p (s c)"),
                                              rhs=nCAi[:, gl, 0:8, :].rearrange("p k c -> p (k c)"), start=False, stop=True),
                     reads=[t_gb], writes=[tps])
                P.op("dve", lambda e: e.tensor_tensor(out=mtmp2[mb_][:, :], in0=ps[:, 0:128], in1=maskM, op=ALU.mult), reads=[tps, t_cst], writes=[t_mt[mb_]])
                P.op("dve", lambda e: e.scalar_tensor_tensor(out=Mg[:, g, :], in0=cst_sb[:, 0:128], scalar=dcol[:, g:g + 1], in1=mtmp2[mb_][:, :],
                                                             op0=ALU.mult, op1=ALU.add), reads=[t_mt[mb_], t_cst, t_gb], writes=[t_Mgw[mb_]])
                for (dstW, srcW) in ((Wr, Wtr), (Wi, Wti)):
                    ps, tps = next_ps()
                    P.op("pe", lambda e: e.transpose(out=ps[0:128, 0:64], in_=srcW[:, gl, :, :].rearrange("p s c -> p (s c)"), identity=cst_sb[0:64, 0:64]),
                         reads=[t_gb, t_cst], writes=[tps])
                    P.op("act", lambda e: e.copy(out=dstW[:, g, :], in_=ps[0:128, 0:64]), reads=[tps], writes=[t_Ww[mb_]])
            P.op("dve", lambda e: e.tensor_copy(out=gbflag[:, 1:2], in_=gbflag[:, 0:1]), reads=[t_gb], writes=[t_su, t_gb])
        A.release(m3)
        P.barrier()
        if cfg.get("dbg_s5m", False):
            o_M = outp("o_M", [128, 64, 128], BF16); o_W = outp("o_W", [128, 64, 64], BF16); o_Q = outp("o_Q", [64, 64, 128], BF16)
            P.dma("sp", o_M[:, :, :], Mg[:, :, :], reads=[t_su]); P.dma("sp", o_W[:, :, :], Wr[:, :, :], reads=[t_su]); P.dma("sp", o_Q[:, :, :], Qst[0:64, :, :], reads=[t_su])
        m4 = A.mark()
        Ug = A.alloc([128, 64, 272], BF16, "Ug"); t_Ug = TT("Ug")
        for q4 in range(4):
            P.dma("sp", Ug[:, q4 * 16:(q4 + 1) * 16, :], ug_d[:, q4 * 16:(q4 + 1) * 16, :], reads=[t_ugd], writes=[t_Ug])
        Hst = A.alloc([128, 64, 160], BF16, "Hst"); t_Hin = TT("Hin")
        Hin_r = Hst[0:64, :, :]; Hin_i = Hst[64:128, :, :]
        Hr = [A.alloc([64, 64], F32, "Hr%d" % i) for i in range(2)]; Hi = [A.alloc([64, 64], F32, "Hi%d" % i) for i in range(2)]
        t_Hr = [TT("Hr%d" % i) for i in range(2)]; t_Hi = [TT("Hi%d" % i) for i in range(2)]
        sc1 = A.alloc([64, 64], F32, "sc1"); sc2 = A.alloc([64, 64], F32, "sc2"); sc3 = A.alloc([64, 64], F32, "sc3"); sc4 = A.alloc([64, 64], F32, "sc4")
        t_scd = TT("scd"); t_scp = TT("scp")
        XB = 16
        mX = A.mark()
        Xr = [A.alloc([64, 64, XB], F32, "Xr%d" % i) for i in range(2)]; Xi = [A.alloc([64, 64, XB], F32, "Xi%d" % i) for i in range(2)]
        t_Xr = [TT("Xr%d" % i) for i in range(2)]; t_Xi = [TT("Xi%d" % i) for i in range(2)]
        P.op("pool", lambda e: e.memset(Hr[0][:, :], 0.0), writes=[t_Hr[0]])
        P.op("pool", lambda e: e.memset(Hi[0][:, :], 0.0), writes=[t_Hi[0]])

        def xblock(blk):
            xb = blk % 2
            j0 = blk * XB
            ncol = XB
            for (Wm, Xd, t_Xd) in ((Wr, Xr[xb], t_Xr[xb]), (Wi, Xi[xb], t_Xi[xb])):
                for g32 in range(2):
                    ps, tps = next_ps()
                    for gl in range(32):
                        g = g32 * 32 + gl
                        P.op("pe", lambda e: e.matmul(ps[0:64, gl * XB:(gl + 1) * XB], lhsT=Wm[:, g, :], rhs=Ug[:, g, j0:j0 + ncol],
                                                      start=True, stop=True), reads=[t_su, t_Ug], writes=[tps])
                    P.op("act", lambda e: e.copy(out=Xd[:, g32 * 32:(g32 + 1) * 32, :],
                                                 in_=ps[0:64, 0:512].rearrange("p (g j) -> p g j", g=32)),
                         reads=[tps], writes=[t_Xd])
            return xb

        cur = 0
        for blk in range(256 // XB):
            xb = xblock(blk)
            for jj in range(XB):
                j = blk * XB + jj
                nx = 1 - cur
                if j >= 112:
                    P.op("act", lambda e: e.copy(out=Hin_r[:, :, j - 112], in_=Hr[cur][:, :]), reads=[t_Hr[cur]], writes=[t_Hin])
                    P.op("act", lambda e: e.copy(out=Hin_i[:, :, j - 112], in_=Hi[cur][:, :]), reads=[t_Hi[cur]], writes=[t_Hin])
                P.op("dve", lambda e: e.tensor_tensor(out=sc1[:, :], in0=A8r[:, :], in1=Hr[cur][:, :], op=ALU.mult), reads=[t_su, t_Hr[cur]], writes=[t_scd])
                P.op("dve", lambda e: e.tensor_tensor(out=sc2[:, :], in0=A8i[:, :], in1=Hi[cur][:, :], op=ALU.mult), reads=[t_su, t_Hi[cur]], writes=[t_scd])
                P.op("dve", lambda e: e.tensor_tensor(out=sc1[:, :], in0=sc1[:, :], in1=sc2[:, :], op=ALU.subtract), reads=[t_scd], writes=[t_scd])
                P.op("dve", lambda e: e.tensor_tensor(out=Hr[nx][:, :], in0=sc1[:, :], in1=Xr[xb][:, :, jj], op=ALU.add), reads=[t_scd, t_Xr[xb]], writes=[t_Hr[nx]])
                P.op("pool", lambda e: e.tensor_tensor(out=sc3[:, :], in0=A8r[:, :], in1=Hi[cur][:, :], op=ALU.mult), reads=[t_su, t_Hi[cur]], writes=[t_scp])
                P.op("pool", lambda e: e.tensor_tensor(out=sc4[:, :], in0=A8i[:, :], in1=Hr[cur][:, :], op=ALU.mult), reads=[t_su, t_Hr[cur]], writes=[t_scp])
                P.op("dve", lambda e: e.tensor_tensor(out=sc3[:, :], in0=sc3[:, :], in1=sc4[:, :], op=ALU.add), reads=[t_scp], writes=[t_scp])
                P.op("dve", lambda e: e.tensor_tensor(out=Hi[nx][:, :], in0=sc3[:, :], in1=Xi[xb][:, :, jj], op=ALU.add), reads=[t_scp, t_Xi[xb]], writes=[t_Hi[nx]])
                cur = nx
        fin = A.alloc([64, 64], F32, "fin"); t_fin = TT("fin")
        for (Hsrc, t_Hs, dst) in ((Hr[cur], t_Hr[cur], o_s5r), (Hi[cur], t_Hi[cur], o_s5i)):
            ps, tps = next_ps()
            P.op("pe", lambda e: e.transpose(out=ps[0:64, 0:64], in_=Hsrc[:, :], identity=cst_sb[0:64, 0:64]), reads=[t_Hs, t_cst], writes=[tps])
            P.op("dve", lambda e: e.tensor_copy(out=fin[:, :], in_=ps[0:64, 0:64]), reads=[tps], writes=[t_fin])
            P.dma("sp", dst[:, :], fin[:, :], reads=[t_fin])
        hs_r = A.alloc([64, 16, 64], F32, "hs_r"); hs_i = A.alloc([64, 16, 64], F32, "hs_i"); t_hs = TT("hs")
        he_r = A.alloc([64, 16, 64], F32, "he_r"); he_i = A.alloc([64, 16, 64], F32, "he_i")
        q1 = A.alloc([64, 16, 64], F32, "q1"); q2 = A.alloc([64, 16, 64], F32, "q2")
        sld = A.alloc([128, 64], F32, "sld"); t_sld = TT("sld")
        for (src, dstH) in ((st_re, hs_r), (st_im, hs_i)):
            for r8 in range(8):
                P.dma("sp", sld[:, :], src[r8 * 128:(r8 + 1) * 128, :], writes=[t_sld])
                ps, tps = next_ps()
                P.op("pe", lambda e: e.transpose(out=ps[0:64, 0:128], in_=sld[:, :], identity=cst_sb[:, 0:128]), reads=[t_sld, t_cst], writes=[tps])
                P.op("dve", lambda e: e.tensor_copy(out=dstH[:, r8 * 2:(r8 + 1) * 2, :].rearrange("p b g -> p (b g)"), in_=ps[0:64, 0:128]),
                     reads=[tps], writes=[t_hs])
        xb = xblock(256 // XB)

        def bc(t):
            return t[:, :].unsqueeze(1).broadcast_to([64, 16, 64])

        def cmul_b(outr, outi, cr, ci, R, W):
            P.op("dve", lambda e: e.tensor_tensor(out=q1[:, :, :], in0=hs_r[:, :, :], in1=bc(cr), op=ALU.mult), reads=[t_hs, t_su] + R, writes=[t_hs])
            P.op("dve", lambda e: e.tensor_tensor(out=q2[:, :, :], in0=hs_i[:, :, :], in1=bc(ci), op=ALU.mult), reads=[t_hs, t_su], writes=[t_hs])
            P.op("dve", lambda e: e.tensor_tensor(out=outr, in0=q1[:, :, :], in1=q2[:, :, :], op=ALU.subtract), reads=[t_hs], writes=[t_hs] + W)
            P.op("dve", lambda e: e.tensor_tensor(out=q1[:, :, :], in0=hs_r[:, :, :], in1=bc(ci), op=ALU.mult), reads=[t_hs, t_su], writes=[t_hs])
            P.op("dve", lambda e: e.tensor_tensor(out=q2[:, :, :], in0=hs_i[:, :, :], in1=bc(cr), op=ALU.mult), reads=[t_hs, t_su], writes=[t_hs])
            P.op("dve", lambda e: e.tensor_tensor(out=outi, in0=q1[:, :, :], in1=q2[:, :, :], op=ALU.add), reads=[t_hs], writes=[t_hs] + W)
        cmul_b(Hin_r[:, :, 144:160].rearrange("p g b -> p b g"), Hin_i[:, :, 144:160].rearrange("p g b -> p b g"), I4r, I4i, [], [t_Hin])
        cmul_b(he_r[:, :, :], he_i[:, :, :], A4r, A4i, [], [])
        P.op("dve", lambda e: e.tensor_tensor(out=he_r[:, :, :], in0=he_r[:, :, :], in1=Xr[xb][:, :, 0:16].rearrange("p g b -> p b g"), op=ALU.add),
             reads=[t_hs, t_Xr[xb]], writes=[t_hs])
        P.op("dve", lambda e: e.tensor_tensor(out=he_i[:, :, :], in0=he_i[:, :, :], in1=Xi[xb][:, :, 0:16].rearrange("p g b -> p b g"), op=ALU.add),
             reads=[t_hs, t_Xi[xb]], writes=[t_hs])
        fin2 = [A.alloc([128, 64], F32, "fin2_%d" % i) for i in range(2)]; t_fin2 = [TT("fin2_%d" % i) for i in range(2)]
        fc_ = 0
        for (Hsrc, dst) in ((he_r, o_s5rs), (he_i, o_s5is)):
            for r8 in range(8):
                fb = fc_ % 2
                fc_ += 1
                ps, tps = next_ps()
                P.op("pe", lambda e: e.transpose(out=ps[:, 0:64], in_=Hsrc[:, r8 * 2:(r8 + 1) * 2, :].rearrange("p b g -> p (b g)"), identity=cst_sb[0:64, 0:64]),
                     reads=[t_hs, t_cst], writes=[tps])
                P.op("dve", lambda e: e.tensor_copy(out=fin2[fb][:, :], in_=ps[:, 0:64]), reads=[tps], writes=[t_fin2[fb]])
                P.dma("sp", dst[r8 * 128:(r8 + 1) * 128, :], fin2[fb][:, :], reads=[t_fin2[fb]])
        P.barrier()
        A.release(mX)
        ga = [A.alloc([128, 480], F32, "ga%d" % i) for i in range(2)]; gb_ = [A.alloc([128, 480], F32, "gb%d" % i) for i in range(2)]
        t_ga = [TT("ga%d" % i) for i in range(2)]
        for g3 in range(22):
            gs_ = list(range(g3 * 3, min(64, g3 * 3 + 3)))
            ncol = len(gs_) * 160
            ps, tps = next_ps()
            for n_, g in enumerate(gs_):
                c0 = n_ * 160
                P.op("pe", lambda e: e.matmul(ps[:, c0:c0 + 160], lhsT=Mg[:, g, :], rhs=Ug[:, g, 112:272], start=True, stop=False),
                     reads=[t_su, t_Ug], writes=[tps])
                P.op("pe", lambda e: e.matmul(ps[:, c0:c0 + 160], lhsT=Qst[:, g, :], rhs=Hst[:, g, :], start=False, stop=True),
                     reads=[t_su, t_Hin], writes=[tps])
            b2 = g3 % 2
            P.op("act", lambda e: e.activation(out=ga[b2][:, 0:ncol], in_=ps[:, 0:ncol], func=AF.Square), reads=[tps], writes=[t_ga[b2]])
            P.op("act", lambda e: e.activation(out=gb_[b2][:, 0:ncol], in_=ps[:, 0:ncol], func=AF.Copy, scale=0.5), reads=[tps], writes=[t_ga[b2]])
            P.op("dve", lambda e: e.tensor_scalar(out=ga[b2][:, 0:ncol], in0=ga[b2][:, 0:ncol], scalar1=0.044715, scalar2=1.0, op0=ALU.mult, op1=ALU.add),
                 reads=[t_ga[b2]], writes=[t_ga[b2]])
            P.op("dve", lambda e: e.tensor_tensor(out=ga[b2][:, 0:ncol], in0=ga[b2][:, 0:ncol], in1=ps[:, 0:ncol], op=ALU.mult),
                 reads=[t_ga[b2], tps], writes=[t_ga[b2]])
            P.op("act", lambda e: e.activation(out=ga[b2][:, 0:ncol], in_=ga[b2][:, 0:ncol], func=AF.Tanh, scale=0.7978845608028654),
                 reads=[t_ga[b2]], writes=[t_ga[b2]])
            P.op("dve", lambda e: e.scalar_tensor_tensor(out=Zg[:, gs_[0]:gs_[-1] + 1, :].rearrange("p g j -> p (g j)"), in0=ga[b2][:, 0:ncol], scalar=1.0,
                                                         in1=gb_[b2][:, 0:ncol], op0=ALU.add, op1=ALU.mult),
                 reads=[t_ga[b2]], writes=[t_Zg])
        A.release(mS)
        P.barrier()
        ssmT = A.alloc_top([128, 8, NTOK], BF16, "ssmT")
        o_ssm_dbg = outp("o_ssm_dbg", [128, 8, NTOK], BF16) if cfg.get("dbg_ssm", False) else None
        ztok = A.alloc([128, 8, 1024], BF16, "ztok"); t_ztok = TT("ztok")
        zT = A.alloc([128, 8, NTOK], BF16, "zT"); t_zT = TT("zT")
        wgl = A.alloc([128, 8, 1024], BF16, "wgl"); t_wgl = TT("wgl")
        wgst = [A.alloc([128, 1024], F32, "wgst%d" % i) for i in range(2)]; t_wgst = [TT("wgst%d" % i) for i in range(2)]
        bgl8 = A.alloc([8, 128], F32, "bgl8"); bglT = A.alloc([128, 8], F32, "bglT"); t_bgl = TT("bgl")
        for fc in range(8):
            b2 = fc % 2
            P.dma("sp", wgst[b2][:, :], w_glu[fc * 128:(fc + 1) * 128, :], writes=[t_wgst[b2]])
            P.op("act", lambda e: e.copy(out=wgl[:, fc, :], in_=wgst[b2][:, :]), reads=[t_wgst[b2]], writes=[t_wgl])
        P.dma("sp", bgl8[:, :], b_glu[:, :], writes=[t_bgl])
        ps, tps = next_ps()
        P.op("pe", lambda e: e.transpose(out=ps[:, 0:8], in_=bgl8[:, :], identity=cst_sb[0:8, 0:8]), reads=[t_bgl, t_cst], writes=[tps])
        P.op("dve", lambda e: e.tensor_copy(out=bglT[:, :], in_=ps[:, 0:8]), reads=[tps], writes=[t_bgl])
        for blk, (c0, nj) in enumerate(((0, 128), (128, 32))):
            for g8 in range(8):
                ps, tps = next_ps()
                psb = bf_view(ps)
                for gg in range(8):
                    g = g8 * 8 + gg
                    P.op("pe", lambda e: e.transpose(out=psb[0:nj, gg * 128:(gg + 1) * 128], in_=Zg[:, g, c0:c0 + nj], identity=ident_b[:, :]),
                         reads=[t_Zg, t_identb], writes=[tps])
                P.op("act" if g8 % 2 == 0 else "dve",
                     (lambda e: e.copy(out=ztok[0:nj, :, g8 * 128:(g8 + 1) * 128].rearrange("j t (g c) -> j g t c", g=8),
                                       in_=psb[0:nj, 0:1024].rearrange("j (g t c) -> j g t c", g=8, t=8)))
                     if g8 % 2 == 0 else
                     (lambda e: e.tensor_copy(out=ztok[0:nj, :, g8 * 128:(g8 + 1) * 128].rearrange("j t (g c) -> j g t c", g=8),
                                              in_=psb[0:nj, 0:1024].rearrange("j (g t c) -> j g t c", g=8, t=8))),
                     reads=[tps], writes=[t_ztok])
            for fc in range(8):
                ps, tps = next_ps()
                psb = bf_view(ps)
                for t_ in range(8):
                    P.op("pe", lambda e: e.transpose(out=psb[:, t_ * nj:(t_ + 1) * nj], in_=ztok[0:nj, t_, fc * 128:(fc + 1) * 128], identity=ident_b[0:nj, 0:nj]),
                         reads=[t_ztok, t_identb], writes=[tps])
                if blk == 0:
                    P.op("act" if fc % 2 == 0 else "dve",
                         (lambda e: e.copy(out=zT[:, fc, 0:1024].rearrange("p (j t) -> p t j", t=8), in_=psb[:, 0:1024].rearrange("p (t j) -> p t j", t=8)))
                         if fc % 2 == 0 else
                         (lambda e: e.tensor_copy(out=zT[:, fc, 0:1024].rearrange("p (j t) -> p t j", t=8), in_=psb[:, 0:1024].rearrange("p (t j) -> p t j", t=8))),
                         reads=[tps], writes=[t_zT])
                else:
                    pv = psb[:, 0:256].rearrange("p (t j) -> p t j", t=8)
                    P.op("act", lambda e: e.copy(out=zT[:, fc, 1024:1152].rearrange("p (j t) -> p t j", t=8), in_=pv[:, :, 0:16]),
                         reads=[tps], writes=[t_zT])
                    P.op("dve", lambda e: e.tensor_copy(out=zT[:, fc, 1152:1216].rearrange("p (b t) -> p t b", t=4), in_=pv[:, 4:8, 16:32]),
                         reads=[tps], writes=[t_zT])
        sig = [A.alloc([128, 512], BF16, "sig%d" % i) for i in range(2)]; t_sig = [TT("sig%d" % i) for i in range(2)]
        sgc = 0
        for oc in range(8):
            for (tb0, tbn) in ((0, 512), (512, 512), (1024, 192)):
                ps, tps = next_ps()
                for fc in range(8):
                    P.op("pe", lambda e: e.matmul(ps[:, 0:tbn], lhsT=wgl[:, fc, oc * 128:(oc + 1) * 128], rhs=zT[:, fc, tb0:tb0 + tbn],
                                                  start=(fc == 0), stop=(fc == 7)), reads=[t_wgl, t_zT], writes=[tps])
                sb_ = sgc % 2
                sgc += 1
                P.op("act", lambda e: e.activation(out=sig[sb_][:, 0:tbn], in_=ps[:, 0:tbn], func=AF.Sigmoid, bias=bglT[:, oc:oc + 1]),
                     reads=[tps, t_bgl], writes=[t_sig[sb_]])
                P.op("dve", lambda e: e.tensor_tensor(out=ssmT[:, oc, tb0:tb0 + tbn], in0=sig[sb_][:, 0:tbn], in1=zT[:, oc, tb0:tb0 + tbn], op=ALU.mult),
                     reads=[t_sig[sb_], t_zT], writes=[t_ssmT])
        if o_ssm_dbg is not None:
            P.dma("sp", o_ssm_dbg[:, :, :], ssmT[:, :, :], reads=[t_ssmT])
        A.release(mS)
        P.barrier()
    if cfg.get("tail", True):
        w_out = inp("w_out", [D, D])
        ffn_w1 = inp("ffn_w1", [2, D, DFF]); ffn_w3 = inp("ffn_w3", [2, D, DFF]); ffn_w2 = inp("ffn_w2", [2, DFF, D])
        pool_w = inp("pool_w", [4, 512, 512]); pool_scale = inp("pool_scale", [1, D])
        poolA = inp("poolA", [128, 12, 128]); poolS = inp("poolS", [128, 12, 64]); state_pool = inp("state_pool", [16, 15, D])
        o_y = outp("o_y", [1024, D]); o_ys = outp("o_ys", [64, D]); o_poolp = outp("o_poolp", [15, D]); o_pools = outp("o_pools", [16, 15, D])
        A.top = consts_top
        P.barrier()
        X1 = A.alloc([128, NA + 1, D], F32, "X1")
        t_X1 = [TT("X1_%d" % i) for i in range(NA + 1)]
        mW = A.mark()
        for ai in range(NA + 1):
            rows = 128 if ai < NA else 64
            src = xw[(ACT0 + ai) * 128:(ACT0 + ai + 1) * 128, :] if ai < NA else xs[:, :]
            P.dma("sp", X1[0:rows, ai, :], src, writes=[t_X1[ai]])
        wob = [A.alloc([128, 16, 512], BF16, "wob%d" % i) for i in range(2)]; t_wob = [TT("wob%d" % i) for i in range(2)]
        wos = [A.alloc([128, 4, 512], F32, "wos%d" % i) for i in range(2)]; t_wos = [TT("wos%d" % i) for i in range(2)]
        w_out_v = w_out.rearrange("(kc p) c -> p kc c", p=128)
        wc_ = 0
        for n in range(4):
            wb_ = n % 2
            for q4 in range(4):
                sb_ = wc_ % 2
                wc_ += 1
                P.dma("sp", wos[sb_][:, :, :], w_out_v[:, q4 * 4:(q4 + 1) * 4, n * 512:(n + 1) * 512], writes=[t_wos[sb_]])
                P.op("act", lambda e: e.copy(out=wob[wb_][:, q4 * 4:(q4 + 1) * 4, :], in_=wos[sb_][:, :, :]), reads=[t_wos[sb_]], writes=[t_wob[wb_]])
            for ai in range(NA + 1):
                rows = 128 if ai < NA else 64
                ps, tps = next_ps()
                for kc in range(16):
                    srcT = attT if kc < 8 else ssmT
                    P.op("pe", lambda e: e.matmul(ps[0:rows, 0:512], lhsT=srcT[:, kc % 8, ai * 128:ai * 128 + rows], rhs=wob[wb_][:, kc, :],
                                                  start=(kc == 0), stop=(kc == 15)),
                         reads=[t_attT[ai], t_ssmT, t_wob[wb_]], writes=[tps])
                P.op("dve", lambda e: e.tensor_tensor(out=X1[0:rows, ai, n * 512:(n + 1) * 512], in0=ps[0:rows, 0:512],
                                                      in1=X1[0:rows, ai, n * 512:(n + 1) * 512], op=ALU.add),
                     reads=[tps, t_X1[ai]], writes=[t_X1[ai]])
        P.barrier()
        A.release(mW)
        A.hi = nc.SBUF_PARTITION_SIZE_BYTES - 1024

        def ffn_layer(L, tiles):
            mF = A.mark()
            nwb = A.alloc([128, D], F32, "nfw"); t_nwb = TT("nfw")
            P.dma("sp", nwb[:, :], nfw[L:L + 1, :].partition_broadcast(128), writes=[t_nwb])
            hT2 = A.alloc([128, 16, NTOK], BF16, "hT2"); t_hT2 = [TT("hT2_%d" % i) for i in range(NA + 1)]
            hb2 = A.alloc([128, D], BF16, "hb2"); t_hb2 = TT("hb2")
            sm2 = A.alloc([128, 8], F32, "sm2"); t_sm2 = TT("sm2")
            for ai in tiles:
                rows = 128 if ai < NA else 64
                rmsnorm_tile(X1[:, ai, :], t_X1[ai], rows, nwb, t_nwb, hb2, t_hb2, hb2, t_hb2, sm2, t_sm2)
                transpose_to(hT2, t_hT2[ai], ai * 128, hb2, t_hb2, rows)
            c_lo = tiles[0] * 128
            c_hi = NTOK
            tbs = []
            c = c_lo
            while c < c_hi:
                n_ = min(512, c_hi - c)
                tbs.append((c, n_))
                c += n_
            FG = 4
            gT = [A.alloc([128, FG, NTOK], BF16, "gT%d" % i) for i in range(2)]; t_gT = [TT("gT%d" % i) for i in range(2)]
            w1b = [A.alloc([128, 16, 128], BF16, "w1b%d" % i) for i in range(2)]; t_w1b = [TT("w1b%d" % i) for i in range(2)]
            w3b = [A.alloc([128, 16, 128], BF16, "w3b%d" % i) for i in range(2)]; t_w3b = [TT("w3b%d" % i) for i in range(2)]
            w2b = [A.alloc([128, FG, 512], BF16, "w2b%d" % i) for i in range(2)]; t_w2b = [TT("w2b%d" % i) for i in range(2)]
            wst_ = [A.alloc([128, 2048], F32, "fst%d" % i) for i in range(3)]; t_wst_ = [TT("fst%d" % i) for i in range(3)]
            slt = [A.alloc([128, 512], BF16, "slt%d" % i) for i in range(2)]; t_slt = [TT("slt%d" % i) for i in range(2)]
            w1v = ffn_w1[L].rearrange("(kc p) f -> p kc f", p=128); w3v = ffn_w3[L].rearrange("(kc p) f -> p kc f", p=128)
            w2v = ffn_w2[L].rearrange("(fc p) c -> p fc c", p=128)
            stc = [0]; slc = [0]; wcc = [0]; w2c = [0]

            def load_cast(dst, t_dst, src_ap, shape3):
                sb_ = stc[0] % 3
                stc[0] += 1
                view = wst_[sb_][:, 0:shape3[0] * shape3[1]].rearrange("p (a b) -> p a b", a=shape3[0])
                P.dma("sp", view, src_ap, writes=[t_wst_[sb_]])
                P.op("act", lambda e: e.copy(out=dst, in_=view), reads=[t_wst_[sb_]], writes=[t_dst])
            for grp in range(DFF // 128 // FG):
                gb2 = grp % 2
                for fl in range(FG):
                    f = grp * FG + fl
                    wb_ = wcc[0] % 2
                    wcc[0] += 1
                    load_cast(w1b[wb_][:, :, :], t_w1b[wb_], w1v[:, :, f * 128:(f + 1) * 128], (16, 128))
                    load_cast(w3b[wb_][:, :, :], t_w3b[wb_], w3v[:, :, f * 128:(f + 1) * 128], (16, 128))
                    for (tb0, tbn) in tbs:
                        psA, tpsA = next_ps()
                        psB, tpsB = next_ps()
                        t_in = [t_hT2[a] for a in tiles if a * 128 < tb0 + tbn and (a + 1) * 128 > tb0]
                        for kc in range(16):
                            P.op("pe", lambda e: e.matmul(psA[:, 0:tbn], lhsT=w1b[wb_][:, kc, :], rhs=hT2[:, kc, tb0:tb0 + tbn], start=(kc == 0), stop=(kc == 15)),
                                 reads=t_in + [t_w1b[wb_]], writes=[tpsA])
                        for kc in range(16):
                            P.op("pe", lambda e: e.matmul(psB[:, 0:tbn], lhsT=w3b[wb_][:, kc, :], rhs=hT2[:, kc, tb0:tb0 + tbn], start=(kc == 0), stop=(kc == 15)),
                                 reads=t_in + [t_w3b[wb_]], writes=[tpsB])
                        sl_ = slc[0] % 2
                        slc[0] += 1
                        P.op("act", lambda e: e.activation(out=slt[sl_][:, 0:tbn], in_=psA[:, 0:tbn], func=AF.Silu), reads=[tpsA], writes=[t_slt[sl_]])
                        P.op("dve", lambda e: e.tensor_tensor(out=gT[gb2][:, fl, tb0:tb0 + tbn], in0=slt[sl_][:, 0:tbn], in1=psB[:, 0:tbn], op=ALU.mult),
                             reads=[t_slt[sl_], tpsB], writes=[t_gT[gb2]])
                for n in range(4):
                    wb2 = w2c[0] % 2
                    w2c[0] += 1
                    load_cast(w2b[wb2][:, :, :], t_w2b[wb2], w2v[:, grp * FG:(grp + 1) * FG, n * 512:(n + 1) * 512], (FG, 512))
                    for ai in tiles:
                        rows = 128 if ai < NA else 64
                        ps, tps = next_ps()
                        for fl in range(FG):
                            P.op("pe", lambda e: e.matmul(ps[0:rows, 0:512], lhsT=gT[gb2][:, fl, ai * 128:ai * 128 + rows], rhs=w2b[wb2][:, fl, :],
                                                          start=(fl == 0), stop=(fl == FG - 1)),
                                 reads=[t_gT[gb2], t_w2b[wb2]], writes=[tps])
                        P.op("dve", lambda e: e.tensor_tensor(out=X1[0:rows, ai, n * 512:(n + 1) * 512], in0=ps[0:rows, 0:512],
                                                              in1=X1[0:rows, ai, n * 512:(n + 1) * 512], op=ALU.add),
                             reads=[tps, t_X1[ai]], writes=[t_X1[ai]])
            P.barrier()
            A.release(mF)

        ffn_layer(0, list(range(NA + 1)))
        if cfg.get("dbg_x", False):
            o_x2 = outp("o_x2", [128, NA + 1, D])
            P.dma("sp", o_x2[:, :, :], X1[:, :, :], reads=t_X1)
        mP = A.mark()
        nwb1 = A.alloc([128, D], F32, "nmw1"); t_nwb1 = TT("nmw1")
        P.dma("sp", nwb1[:, :], nmw[1:2, :].partition_broadcast(128), writes=[t_nwb1])
        psc = A.alloc([128, D], F32, "psc"); t_psc = TT("psc")
        P.dma("sp", psc[:, :], pool_scale[0:1, :].partition_broadcast(128), writes=[t_psc])
        pA32 = A.alloc([128, 12, 128], F32, "pA32"); pA = A.alloc([128, 12, 128], BF16, "pA"); t_pA = TT("pA")
        P.dma("sp", pA32[:, :, :], poolA[:, :, :], writes=[t_pA])
        P.op("dve", lambda e: e.tensor_copy(out=pA[:, :, :], in_=pA32[:, :, :]), reads=[t_pA], writes=[t_pA])
        pS32 = A.alloc([128, 12, 64], F32, "pS32"); pS = A.alloc([128, 12, 64], BF16, "pS"); t_pS = TT("pS")
        P.dma("sp", pS32[:, :, :], poolS[:, :, :], writes=[t_pS])
        P.op("dve", lambda e: e.tensor_copy(out=pS[:, :, :], in_=pS32[:, :, :]), reads=[t_pS], writes=[t_pS])
        pwb = A.alloc([128, 16, 512], BF16, "pwb"); t_pwb = TT("pwb")
        pwst = [A.alloc([128, 4, 512], F32, "pwst%d" % i) for i in range(2)]; t_pwst = [TT("pwst%d" % i) for i in range(2)]
        for g in range(4):
            P.dma("sp", pwst[g % 2][:, :, :], pool_w[g].rearrange("(fc p) c -> p fc c", p=128), writes=[t_pwst[g % 2]])
            P.op("act", lambda e: e.copy(out=pwb[:, g * 4:(g + 1) * 4, :], in_=pwst[g % 2][:, :, :]), reads=[t_pwst[g % 2]], writes=[t_pwb])
        hp32 = A.alloc([128, D], F32, "hp32"); t_hp32 = TT("hp32")
        hpb = [A.alloc([128, D], BF16, "hpb%d" % i) for i in range(2)]; t_hpb = [TT("hpb%d" % i) for i in range(2)]
        sm3 = A.alloc([128, 8], F32, "sm3"); t_sm3 = TT("sm3")
        plT = A.alloc([128, 16, 128], BF16, "plT"); t_plT = TT("plT")
        ptmp = A.alloc([128, 512], F32, "ptmp"); t_ptmp = TT("ptmp")
        spb = [A.alloc([120, D], BF16, "spb%d" % i) for i in range(2)]; t_spb = TT("spb")
        sp32 = A.alloc([120, D], F32, "sp32"); t_sp32 = TT("sp32")

        def norm1(ai, rows, outb, t_outb):
            P.op("act", lambda e: e.activation(out=outb[0:rows, :], in_=X1[0:rows, ai, :], func=AF.Square, accum_out=sm3[0:rows, 0:1]),
                 reads=[t_X1[ai]], writes=[t_outb, t_sm3])
            P.op("act", lambda e: e.activation(out=sm3[0:rows, 1:2], in_=sm3[0:rows, 0:1], func=AF.Sqrt, scale=1.0 / D, bias=eps_ref[0][0:rows, 0:1]),
                 reads=[t_sm3, t_cst2], writes=[t_sm3])
            P.op("dve", lambda e: e.reciprocal(out=sm3[0:rows, 2:3], in_=sm3[0:rows, 1:2]), reads=[t_sm3], writes=[t_sm3])
            P.op("dve", lambda e: e.scalar_tensor_tensor(out=hp32[0:rows, :], in0=X1[0:rows, ai, :], scalar=sm3[0:rows, 2:3], in1=nwb1[0:rows, :],
                                                         op0=ALU.mult, op1=ALU.mult), reads=[t_X1[ai], t_sm3, t_nwb1], writes=[t_hp32])
            P.op("act", lambda e: e.copy(out=outb[0:rows, :], in_=hp32[0:rows, :]), reads=[t_hp32], writes=[t_outb])

        def pool_out(ai, rows):
            for g in range(4):
                ps, tps = next_ps()
                for fl in range(4):
                    P.op("pe", lambda e: e.matmul(ps[0:rows, 0:512], lhsT=plT[:, g * 4 + fl, 0:rows], rhs=pwb[:, g * 4 + fl, :], start=(fl == 0), stop=(fl == 3)),
                         reads=[t_plT, t_pwb], writes=[tps])
                P.op("dve", lambda e: e.tensor_tensor(out=ptmp[0:rows, :], in0=ps[0:rows, 0:512], in1=psc[0:rows, g * 512:(g + 1) * 512], op=ALU.mult),
                     reads=[tps, t_psc], writes=[t_ptmp])
                P.op("dve", lambda e: e.tensor_tensor(out=X1[0:rows, ai, g * 512:(g + 1) * 512], in0=ptmp[0:rows, :], in1=X1[0:rows, ai, g * 512:(g + 1) * 512], op=ALU.add),
                     reads=[t_ptmp, t_X1[ai]], writes=[t_X1[ai]])

        for ai in range(NA):
            cb_, pb_ = ai % 2, (ai + 1) % 2
            norm1(ai, 128, hpb[cb_], t_hpb[cb_])
            if ai == NA - 1:
                P.dma("sp", o_poolp[:, :], hp32[113:128, :], reads=[t_hp32])
            if ai == 0:
                continue
            for q4 in range(4):
                ps, tps = next_ps()
                psb = bf_view(ps)
                for j in range(4):
                    fc = q4 * 4 + j
                    wi_ = fc // 4
                    acur = pA[:, (8 + wi_) if ai == 1 else wi_, :]
                    P.op("pe", lambda e: e.matmul(ps[:, j * 128:(j + 1) * 128], lhsT=hpb[cb_][:, fc * 128:(fc + 1) * 128], rhs=acur, start=True, stop=False),
                         reads=[t_hpb[cb_], t_pA], writes=[tps])
                    P.op("pe", lambda e: e.matmul(ps[:, j * 128:(j + 1) * 128], lhsT=hpb[pb_][:, fc * 128:(fc + 1) * 128], rhs=pA[:, 4 + wi_, :], start=False, stop=True),
                         reads=[t_hpb[pb_], t_pA], writes=[tps])
                P.op("act", lambda e: e.copy(out=plT[:, q4 * 4:(q4 + 1) * 4, :], in_=ps[:, 0:512].rearrange("p (j t) -> p j t", j=4)), reads=[tps], writes=[t_plT])
            pool_out(ai, 128)
        norm1(NA, 64, hpb[0], t_hpb[0])
        P.dma("sp", o_pools[:, 11:15, :], hp32[0:64, :], reads=[t_hp32])
        P.dma("sp", o_pools[:, 0:11, :], state_pool[:, 4:15, :])
        for hf_ in range(2):
            P.dma("sp", sp32[:, :], state_pool[hf_ * 8:(hf_ + 1) * 8, :, :].rearrange("b r d -> (b r) d"), writes=[t_sp32])
            P.op("pool", lambda e: e.tensor_copy(out=spb[hf_][:, :], in_=sp32[:, :]), reads=[t_sp32], writes=[t_spb])
        for q4 in range(4):
            ps, tps = next_ps()
            for j in range(4):
                fc = q4 * 4 + j
                wi_ = fc // 4
                P.op("pe", lambda e: e.matmul(ps[:, j * 64:(j + 1) * 64], lhsT=hpb[0][0:64, fc * 128:(fc + 1) * 128], rhs=pS[0:64, wi_, :], start=True, stop=False),
                     reads=[t_hpb[0], t_pS], writes=[tps])
                P.op("pe", lambda e: e.matmul(ps[:, j * 64:(j + 1) * 64], lhsT=spb[0][:, fc * 128:(fc + 1) * 128], rhs=pS[0:120, 4 + wi_, :], start=False, stop=False),
                     reads=[t_spb, t_pS], writes=[tps])
                P.op("pe", lambda e: e.matmul(ps[:, j * 64:(j + 1) * 64], lhsT=spb[1][:, fc * 128:(fc + 1) * 128], rhs=pS[0:120, 8 + wi_, :], start=False, stop=True),
                     reads=[t_spb, t_pS], writes=[tps])
            P.op("act", lambda e: e.copy(out=plT[:, q4 * 4:(q4 + 1) * 4, 0:64], in_=ps[:, 0:256].rearrange("p (j t) -> p j t", j=4)), reads=[tps], writes=[t_plT])
        pool_out(NA, 64)
        P.barrier()
        A.release(mP)
        ffn_layer(1, list(range(1, NA + 1)))
        for ai in range(1, NA):
            P.dma("sp", o_y[(ai - 1) * 128:ai * 128, :], X1[:, ai, :], reads=[t_X1[ai]])
        P.dma("sp", o_ys[:, :], X1[0:64, NA, :], reads=[t_X1[NA]])
    P.barrier()
    P.emit()
    return nc, di, do


def host_consts():
    c = np.zeros((128, 512), np.float32)
    c[:, 0:128] = np.eye(128, dtype=np.float32)
    c[:, 128:256] = np.triu(np.ones((128, 128), np.float32))
    sidx = np.arange(128) // 16
    c[:, 256:384] = (sidx[None, :] >= sidx[:, None]).astype(np.float32)
    c[:, 384:512] = np.tril(np.ones((128, 128), np.float32), -1)
    return c


def pool_consts(first_tile_is_seq_start):
    wins = (2, 4, 8, 16)
    a = np.zeros((128, 12, 128), np.float32)
    t = np.arange(128)
    for wi, w in enumerate(wins):
        dlt = t[None, :] - t[:, None]
        band = ((dlt >= 0) & (dlt < w)).astype(np.float32)
        a[:, wi, :] = band / w - np.eye(128, dtype=np.float32)
        a[:, 4 + wi, :] = ((t[None, :] + 128 - t[:, None]) < w).astype(np.float32) / w
        if first_tile_is_seq_start:
            cnt = np.minimum(t + 1, w).astype(np.float32)
            a[:, 8 + wi, :] = band / cnt[None, :] - np.eye(128, dtype=np.float32)
        else:
            a[:, 8 + wi, :] = a[:, wi, :]
    s_ = np.zeros((128, 12, 64), np.float32)
    for wi, w in enumerate(wins):
        for b in range(16):
            for tq in range(4):
                col = b * 4 + tq
                for tp in range(4):
                    if 0 <= tq - tp < w:
                        s_[b * 4 + tp, wi, col] += 1.0 / w
                s_[b * 4 + tq, wi, col] -= 1.0
                for r in range(15):
                    if r >= 16 + tq - w:
                        s_[(b % 8) * 15 + r, 4 + 4 * (b // 8) + wi, col] = 1.0 / w
    return a, s_


def host_consts2():
    c = np.zeros((128, 128), np.float32)
    i = np.arange(64)
    c[0:64, 0:64] = ((i[:, None] // 4 == i[None, :] // 4) & (i[:, None] % 4 <= i[None, :] % 4)).astype(np.float32)
    return c


def make_in_maps(inputs, di):
    _host_cache = {}
    xp = np.asarray(inputs["x_prompt"])
    xsm = np.asarray(inputs["x_sample"])
    maps = []
    for c in range(NCORE):
        b, half = c // 2, c % 2
        xw = np.zeros((2048, D), np.float32)
        if half == 1:
            xw[:] = xp[b]
        else:
            xw[1024:] = xp[b, 0:1024]
        m = {"xw": xw, "xs": np.ascontiguousarray(xsm[16 * c:16 * c + 16].reshape(64, D)),
             "cst": host_consts()}
        km = np.zeros((1, 2048), np.float32)
        if half == 0:
            km[0, 0:1024] = -60.0
        m["kmask"] = km
        m["cst2"] = host_consts2()
        if "cache_k" in inputs and "ckv" in di:
            if "ckv" not in _host_cache:
                _host_cache["ckv"] = np.concatenate([np.asarray(inputs["cache_k"])[0].reshape(2560 * 128, NH * HD),
                                                     np.asarray(inputs["cache_v"])[0].reshape(2560 * 128, NH * HD)], axis=1)
            m["ckv"] = _host_cache["ckv"]
            m["clf"] = np.asarray(inputs["cache_logf"])[0].reshape(2560, 128 * NH)
            m["ptab"] = np.ascontiguousarray(np.asarray(inputs["page_table"])[16 * c:16 * c + 16].reshape(1, 256).astype(np.int32))
        for k in ("norm_mix_w", "norm_ffn_w"):
            m[k] = np.ascontiguousarray(np.asarray(inputs[k]))
        m["w_in"] = np.ascontiguousarray(np.asarray(inputs["w_in"])[0])
        m["b_f"] = np.asarray(inputs["b_f"]).reshape(1, NH)
        m["q_norm_w"] = np.asarray(inputs["q_norm_w"]).reshape(1, HD)
        m["k_norm_w"] = np.asarray(inputs["k_norm_w"]).reshape(1, HD)
        for k in ("s5_lam_re", "s5_lam_im", "s5_b_re", "s5_b_im"):
            if k in inputs:
                m[k] = np.ascontiguousarray(np.asarray(inputs[k])[0])
        if "s5_log_dt" in inputs:
            m["s5_log_dt"] = np.asarray(inputs["s5_log_dt"]).reshape(1, 64)
            m["s5_c_re"] = np.ascontiguousarray(np.asarray(inputs["s5_c_re"])[0].reshape(1024, 64))
            m["s5_c_im"] = np.ascontiguousarray(np.asarray(inputs["s5_c_im"])[0].reshape(1024, 64))
            m["s5_d"] = np.ascontiguousarray(np.asarray(inputs["s5_d"])[0])
            m["w_glu"] = np.ascontiguousarray(np.asarray(inputs["w_glu"])[0])
            m["b_glu"] = np.ascontiguousarray(np.asarray(inputs["b_glu"])[0].reshape(8, 128))
            m["st_re"] = np.ascontiguousarray(np.asarray(inputs["state_s5_re"])[0, 16 * c:16 * c + 16].reshape(1024, 64))
            m["st_im"] = np.ascontiguousarray(np.asarray(inputs["state_s5_im"])[0, 16 * c:16 * c + 16].reshape(1024, 64))
        if "pool_w" in inputs:
            m["pool_w"] = np.ascontiguousarray(np.asarray(inputs["pool_w"])[0])
            m["pool_scale"] = np.asarray(inputs["pool_scale"]).reshape(1, D)
            m["state_pool"] = np.ascontiguousarray(np.asarray(inputs["state_pool"])[0, 16 * c:16 * c + 16])
            m["poolA"], m["poolS"] = pool_consts(half == 0)
        if "w_out" in inputs:
            m["w_out"] = np.ascontiguousarray(np.asarray(inputs["w_out"])[0])
            for k in ("ffn_w1", "ffn_w3", "ffn_w2"):
                m[k] = np.asarray(inputs[k])
        maps.append({k: v for k, v in m.items() if k in di})
    return maps


def run(inputs, cfg=None):
    nc, di, do = build_program(cfg or {})
    maps = make_in_maps(inputs, di)
    res = run_bass_kernel_spmd(nc, maps, core_ids=list(range(NCORE)))
    return res.results


def kernel(**inputs):
    res = run(inputs)
    B, L, DB, DS = 4, 2048, 128, 4

    def prompt_rows(name, tail):
        out = np.zeros((B, L) + tail, np.float32)
        for c in range(NCORE):
            b, half = c // 2, c % 2
            out[b, half * 1024:(half + 1) * 1024] = np.asarray(res[c][name]).reshape((1024,) + tail)
        return out

    def sample_rows(name, tail):
        out = np.zeros((DB, DS) + tail, np.float32)
        for c in range(NCORE):
            out[16 * c:16 * c + 16] = np.asarray(res[c][name]).reshape((16, DS) + tail)
        return out

    def prompt_last(name, tail):
        out = np.zeros((1, B) + tail, np.float32)
        for b in range(B):
            out[0, b] = np.asarray(res[2 * b + 1][name]).reshape(tail)
        return out

    def sample_seq(name, tail):
        out = np.zeros((1, DB) + tail, np.float32)
        for c in range(NCORE):
            out[0, 16 * c:16 * c + 16] = np.asarray(res[c][name]).reshape((16,) + tail)
        return out

    y_p = prompt_rows("o_y", (D,))
    y_s = sample_rows("o_ys", (D,))
    k_p = prompt_rows("o_k", (NH, HD))[None]
    v_p = prompt_rows("o_v", (NH, HD))[None]
    lf_p = prompt_rows("o_lf", (NH,))[None]
    s5rp = prompt_last("o_s5r", (64, 64))
    s5ip = prompt_last("o_s5i", (64, 64))
    poolp = prompt_last("o_poolp", (15, D))
    k_s = sample_rows("o_ks", (NH, HD))[None]
    v_s = sample_rows("o_vs", (NH, HD))[None]
    lf_s = sample_rows("o_lfs", (NH,))[None]
    s5rs = sample_seq("o_s5rs", (64, 64))
    s5is = sample_seq("o_s5is", (64, 64))
    pools = sample_seq("o_pools", (15, D))
    return (y_p, y_s, k_p, v_p, lf_p, s5rp, s5ip, poolp, k_s, v_s, lf_s, s5rs, s5is, pools)
```
